# Optimizing a Trainium2 kernel written in Bass

```python
import math
import jax
import jax.numpy as jnp
from jax import lax
import numpy as np

D_MODEL = 2048
BATCH = 4
SEQ = 2048
DEPTH = 1

GRID_W = 64
CTX_LEN = 256
HEAD_DIM = 128
N_Q_HEADS = D_MODEL // HEAD_DIM
N_KV_HEADS = N_Q_HEADS // 4
Q_PER_KV = N_Q_HEADS // N_KV_HEADS
ATTN_W = N_Q_HEADS * HEAD_DIM
KV_W = N_KV_HEADS * HEAD_DIM
ROPE_AXIS_DIM = HEAD_DIM // 2
ROPE_THETA = 10000.0
Q_BLOCK = 128
ATTN_SCALE = HEAD_DIM ** -0.5
SSM_W = D_MODEL // 2
SSM_GROUP = 16
SSM_GROUPS = SSM_W // SSM_GROUP
SSM_STATE = 64
DT_MIN = 1e-3
DT_MAX = 1e-1
D_FF = ((8 * D_MODEL // 3 + 255) // 256) * 256
N_MOD = 9
N_MOD_CTX_LAST = 5
NORM_EPS = 1e-6
CTX_IN_W = 2 * KV_W + SSM_W
IN_W = CTX_IN_W + ATTN_W + 2 * D_MODEL
SPLITS = [KV_W, 2 * KV_W, CTX_IN_W, CTX_IN_W + ATTN_W]

kernel_name = 'hybrid_s5_gqa_macaron_dit_layer'


def _rms_norm(x, g):
    xf = x.astype(jnp.float32)
    xf = xf * lax.rsqrt(jnp.mean(xf * xf, axis=-1, keepdims=True) + NORM_EPS)
    return xf.astype(x.dtype) * g


def _modulate(h, shift, scale):
    return h * (1 + scale) + shift


def _swiglu(h, w_gate, w_up, w_down):
    return (jax.nn.silu(h @ w_gate) * (h @ w_up)) @ w_down


def _axial_rope_tables(L):
    rows = L // GRID_W
    row_ids = jnp.broadcast_to(jnp.arange(rows)[:, None], (rows, GRID_W)).reshape(-1)
    col_ids = jnp.broadcast_to(jnp.arange(GRID_W)[None, :], (rows, GRID_W)).reshape(-1)
    half = ROPE_AXIS_DIM // 2
    inv_freq = ROPE_THETA ** (-jnp.arange(half, dtype=jnp.float32) / half)
    ang_r = row_ids.astype(jnp.float32)[:, None, None] * inv_freq
    ang_c = col_ids.astype(jnp.float32)[:, None, None] * inv_freq
    return (jnp.cos(ang_r), jnp.sin(ang_r), jnp.cos(ang_c), jnp.sin(ang_c))


def _rope_half(x, cos, sin):
    cos = cos.astype(x.dtype)
    sin = sin.astype(x.dtype)
    x1, x2 = jnp.split(x, 2, axis=-1)
    return jnp.concatenate([x1 * cos - x2 * sin, x2 * cos + x1 * sin], axis=-1)


def _axial_rope(x, tables):
    cos_r, sin_r, cos_c, sin_c = tables
    return jnp.concatenate([_rope_half(x[..., :ROPE_AXIS_DIM], cos_r, sin_r),
                            _rope_half(x[..., ROPE_AXIS_DIM:], cos_c, sin_c)], axis=-1)


def _attend_block(qb, k, v):
    B, T = qb.shape[0], qb.shape[1]
    qg = qb.reshape(B, T, N_KV_HEADS, Q_PER_KV, HEAD_DIM)
    s = jnp.einsum('bqkrd,bskd->bkrqs', qg, k).astype(jnp.float32) * ATTN_SCALE
    p = jax.nn.softmax(s, axis=-1).astype(v.dtype)
    o = jnp.einsum('bkrqs,bskd->bqkrd', p, v)
    return o.reshape(B, T, ATTN_W)


def _blocked_attention(q, k, v):
    B, L = q.shape[0], q.shape[1]
    nb = L // Q_BLOCK
    qb = q.reshape(B, nb, Q_BLOCK, N_Q_HEADS, HEAD_DIM).swapaxes(0, 1)
    o = lax.map(lambda qi: _attend_block(qi, k, v), qb)
    return o.swapaxes(0, 1).reshape(B, L, ATTN_W)


def _zoh(a_re, a_im, log_dt):
    a_re = a_re.astype(jnp.float32)
    a_im = a_im.astype(jnp.float32)
    dt = jnp.exp(log_dt.astype(jnp.float32))[:, None]
    mag = jnp.exp(a_re * dt)
    lb_re = mag * jnp.cos(a_im * dt)
    lb_im = mag * jnp.sin(a_im * dt)
    den = a_re * a_re + a_im * a_im
    coef_re = ((lb_re - 1.0) * a_re + lb_im * a_im) / den
    coef_im = (lb_im * a_re - (lb_re - 1.0) * a_im) / den
    return lb_re, lb_im, coef_re, coef_im


def _drive(u, b_re, b_im, coef_re, coef_im):
    bu_re = jnp.einsum('blgc,gpc->blgp', u, b_re)
    bu_im = jnp.einsum('blgc,gpc->blgp', u, b_im)
    return coef_re * bu_re - coef_im * bu_im, coef_re * bu_im + coef_im * bu_re


def _combine(e1, e2):
    a1r, a1i, b1r, b1i = e1
    a2r, a2i, b2r, b2i = e2
    return (a2r * a1r - a2i * a1i,
            a2r * a1i + a2i * a1r,
            a2r * b1r - a2i * b1i + b2r,
            a2r * b1i + a2i * b1r + b2i)


def _scan(lb_re, lb_im, bu_re, bu_im, reverse, h0=None):
    L = bu_re.shape[1]
    a_re = jnp.broadcast_to(lb_re, (1, L) + lb_re.shape)
    a_im = jnp.broadcast_to(lb_im, (1, L) + lb_im.shape)
    A_re, A_im, s_re, s_im = lax.associative_scan(_combine, (a_re, a_im, bu_re, bu_im),
                                                  reverse=reverse, axis=1)
    if h0 is None:
        return s_re, s_im
    h0_re, h0_im = h0
    return (s_re + A_re * h0_re - A_im * h0_im, s_im + A_re * h0_im + A_im * h0_re)


def _readout(h_re, h_im, c_re, c_im):
    return (jnp.einsum('blgp,gcp->blgc', h_re, c_re)
            - jnp.einsum('blgp,gcp->blgc', h_im, c_im))


def _s5_mixer(u, uc, a_re, a_im, log_dt, b_re, b_im, c_re, c_im, d, ctx_out):
    B, L = u.shape[0], u.shape[1]
    Lc = uc.shape[1]
    uf = u.astype(jnp.float32).reshape(B, L, SSM_GROUPS, SSM_GROUP)
    ucf = uc.astype(jnp.float32).reshape(B, Lc, SSM_GROUPS, SSM_GROUP)
    d_g = d.astype(jnp.float32).reshape(SSM_GROUPS, SSM_GROUP)
    y = d_g * uf
    yc = d_g * ucf if ctx_out else None
    for direction, reverse in ((0, False), (1, True)):
        lb_re, lb_im, coef_re, coef_im = _zoh(a_re[direction], a_im[direction], log_dt[direction])
        br = b_re[direction].astype(jnp.float32)
        bi = b_im[direction].astype(jnp.float32)
        cr = c_re[direction].astype(jnp.float32)
        ci = c_im[direction].astype(jnp.float32)
        dc_re, dc_im = _drive(ucf, br, bi, coef_re, coef_im)
        hc_re, hc_im = _scan(lb_re, lb_im, dc_re, dc_im, reverse)
        edge = slice(0, 1) if reverse else slice(Lc - 1, Lc)
        h0 = (hc_re[:, edge], hc_im[:, edge])
        dl_re, dl_im = _drive(uf, br, bi, coef_re, coef_im)
        h_re, h_im = _scan(lb_re, lb_im, dl_re, dl_im, reverse, h0)
        y = y + _readout(h_re, h_im, cr, ci)
        if ctx_out:
            yc = yc + _readout(hc_re, hc_im, cr, ci)
    y = y.reshape(B, L, SSM_W).astype(u.dtype)
    if ctx_out:
        yc = yc.reshape(B, Lc, SSM_W).astype(u.dtype)
    return y, yc


def _merge(attn, ssm, gate, w_glu, b_glu, w_br_attn, w_br_ssm, w_out):
    y = jax.nn.gelu(ssm)
    y = y * jax.nn.sigmoid(y @ w_glu + b_glu)
    g_attn, g_ssm = jnp.split(jax.nn.sigmoid(gate), 2, axis=-1)
    merged = g_attn * (attn @ w_br_attn) + g_ssm * (y @ w_br_ssm)
    return merged @ w_out


def _token_mixer(h, hc, rope, w_in, q_g, k_g, a_re, a_im, log_dt, b_re, b_im, c_re, c_im, d,
                 w_glu, b_glu, w_br_attn, w_br_ssm, w_out, ctx_out):
    B, L = h.shape[0], h.shape[1]
    Lc = hc.shape[1]
    k, v, u, q, gate = jnp.split(h @ w_in, SPLITS, axis=-1)
    pc = hc @ (w_in if ctx_out else w_in[:, :CTX_IN_W])
    kc, vc, uc = pc[..., :KV_W], pc[..., KV_W:2 * KV_W], pc[..., 2 * KV_W:CTX_IN_W]
    q = _axial_rope(_rms_norm(q.reshape(B, L, N_Q_HEADS, HEAD_DIM), q_g), rope)
    k = _axial_rope(_rms_norm(k.reshape(B, L, N_KV_HEADS, HEAD_DIM), k_g), rope)
    kc = _rms_norm(kc.reshape(B, Lc, N_KV_HEADS, HEAD_DIM), k_g)
    vc = vc.reshape(B, Lc, N_KV_HEADS, HEAD_DIM)
    v = v.reshape(B, L, N_KV_HEADS, HEAD_DIM)
    k_all = jnp.concatenate([kc, k], axis=1)
    v_all = jnp.concatenate([vc, v], axis=1)
    attn = _blocked_attention(q, k_all, v_all)
    ssm, ssm_c = _s5_mixer(u, uc, a_re, a_im, log_dt, b_re, b_im, c_re, c_im, d, ctx_out)
    out = _merge(attn, ssm, gate, w_glu, b_glu, w_br_attn, w_br_ssm, w_out)
    out_c = None
    if ctx_out:
        qc = _rms_norm(pc[..., CTX_IN_W:CTX_IN_W + ATTN_W].reshape(B, Lc, N_Q_HEADS, HEAD_DIM), q_g)
        attn_c = _attend_block(qc, kc, vc)
        out_c = _merge(attn_c, ssm_c, pc[..., CTX_IN_W + ATTN_W:], w_glu, b_glu,
                       w_br_attn, w_br_ssm, w_out)
    return out, out_c


def setup_inputs(seed: int = 0) -> dict:
    key = jax.random.key(seed)
    ks = jax.random.split(key, 32)
    f32 = jnp.float32

    def nrm(k, shape, scale):
        return jax.random.normal(k, shape, f32) * scale

    G, P, E = SSM_GROUPS, SSM_STATE, SSM_GROUP
    n_idx = jnp.arange(P, dtype=f32)
    return {
        'x': nrm(ks[0], (BATCH, SEQ, D_MODEL), 1.0),
        'c': nrm(ks[1], (BATCH, D_MODEL), 1.0),
        'ctx': nrm(ks[2], (BATCH, CTX_LEN, D_MODEL), 1.0),
        'c_ctx': nrm(ks[3], (D_MODEL,), 1.0),
        'w_mod': nrm(ks[4], (DEPTH, D_MODEL, N_MOD * D_MODEL), 0.5 * D_MODEL ** -0.5),
        'b_mod': nrm(ks[5], (DEPTH, N_MOD * D_MODEL), 0.01),
        'norm_g': 1.0 + nrm(ks[6], (DEPTH, 3, D_MODEL), 0.02),
        'w_ffn1_gate': nrm(ks[7], (DEPTH, D_MODEL, D_FF), D_MODEL ** -0.5),
        'w_ffn1_up': nrm(ks[8], (DEPTH, D_MODEL, D_FF), D_MODEL ** -0.5),
        'w_ffn1_down': nrm(ks[9], (DEPTH, D_FF, D_MODEL), D_FF ** -0.5),
        'w_in': nrm(ks[10], (DEPTH, D_MODEL, IN_W), D_MODEL ** -0.5),
        'q_norm_g': 1.0 + nrm(ks[11], (DEPTH, HEAD_DIM), 0.02),
        'k_norm_g': 1.0 + nrm(ks[12], (DEPTH, HEAD_DIM), 0.02),
        'ssm_a_re': -0.5 + nrm(ks[13], (DEPTH, 2, G, P), 0.01),
        'ssm_a_im': math.pi * n_idx + nrm(ks[14], (DEPTH, 2, G, P), 0.01),
        'ssm_log_dt': jax.random.uniform(ks[15], (DEPTH, 2, G), f32,
                                         math.log(DT_MIN), math.log(DT_MAX)),
        'ssm_b_re': nrm(ks[16], (DEPTH, 2, G, P, E), (2 * E) ** -0.5),
        'ssm_b_im': nrm(ks[17], (DEPTH, 2, G, P, E), (2 * E) ** -0.5),
        'ssm_c_re': nrm(ks[18], (DEPTH, 2, G, E, P), P ** -0.5),
        'ssm_c_im': nrm(ks[19], (DEPTH, 2, G, E, P), P ** -0.5),
        'ssm_d': nrm(ks[20], (DEPTH, SSM_W), 1.0),
        'w_glu': nrm(ks[21], (DEPTH, SSM_W, SSM_W), SSM_W ** -0.5),
        'b_glu': nrm(ks[22], (DEPTH, SSM_W), 0.01),
        'w_br_attn': nrm(ks[23], (DEPTH, ATTN_W, D_MODEL), ATTN_W ** -0.5),
        'w_br_ssm': nrm(ks[24], (DEPTH, SSM_W, D_MODEL), SSM_W ** -0.5),
        'w_out': nrm(ks[25], (DEPTH, D_MODEL, D_MODEL), D_MODEL ** -0.5),
        'w_ffn2_gate': nrm(ks[26], (DEPTH, D_MODEL, D_FF), D_MODEL ** -0.5),
        'w_ffn2_up': nrm(ks[27], (DEPTH, D_MODEL, D_FF), D_MODEL ** -0.5),
        'w_ffn2_down': nrm(ks[28], (DEPTH, D_FF, D_MODEL), D_FF ** -0.5),
    }


def reference(x, c, ctx, c_ctx, w_mod, b_mod, norm_g, w_ffn1_gate, w_ffn1_up, w_ffn1_down,
              w_in, q_norm_g, k_norm_g, ssm_a_re, ssm_a_im, ssm_log_dt, ssm_b_re, ssm_b_im,
              ssm_c_re, ssm_c_im, ssm_d, w_glu, b_glu, w_br_attn, w_br_ssm, w_out,
              w_ffn2_gate, w_ffn2_up, w_ffn2_down):
    L = x.shape[1]
    rope = _axial_rope_tables(L)
    silu_c = jax.nn.silu(c)
    silu_cc = jax.nn.silu(c_ctx)
    for l in range(DEPTH):
        last = l == DEPTH - 1
        n_ctx_mod = N_MOD_CTX_LAST if last else N_MOD
        mod = (silu_c @ w_mod[l] + b_mod[l])[:, None, :]
        sh1, sc1, g1, sh2, sc2, g2, sh3, sc3, g3 = jnp.split(mod, N_MOD, axis=-1)
        mod_c = silu_cc @ w_mod[l][:, :n_ctx_mod * D_MODEL] + b_mod[l][:n_ctx_mod * D_MODEL]
        mc = jnp.split(mod_c, n_ctx_mod, axis=-1)
        ffn1 = (w_ffn1_gate[l], w_ffn1_up[l], w_ffn1_down[l])
        ffn2 = (w_ffn2_gate[l], w_ffn2_up[l], w_ffn2_down[l])
        x = x + 0.5 * g1 * _swiglu(_modulate(_rms_norm(x, norm_g[l, 0]), sh1, sc1), *ffn1)
        ctx = ctx + 0.5 * mc[2] * _swiglu(_modulate(_rms_norm(ctx, norm_g[l, 0]), mc[0], mc[1]), *ffn1)
        h = _modulate(_rms_norm(x, norm_g[l, 1]), sh2, sc2)
        hc = _modulate(_rms_norm(ctx, norm_g[l, 1]), mc[3], mc[4])
        mix, mix_c = _token_mixer(h, hc, rope, w_in[l], q_norm_g[l], k_norm_g[l],
                                  ssm_a_re[l], ssm_a_im[l], ssm_log_dt[l], ssm_b_re[l], ssm_b_im[l],
                                  ssm_c_re[l], ssm_c_im[l], ssm_d[l], w_glu[l], b_glu[l],
                                  w_br_attn[l], w_br_ssm[l], w_out[l], not last)
        x = x + g2 * mix
        x = x + 0.5 * g3 * _swiglu(_modulate(_rms_norm(x, norm_g[l, 2]), sh3, sc3), *ffn2)
        if not last:
            ctx = ctx + mc[5] * mix_c
            ctx = ctx + 0.5 * mc[8] * _swiglu(_modulate(_rms_norm(ctx, norm_g[l, 2]), mc[6], mc[7]), *ffn2)
    return x
```

```python
import math
import numpy as np
import concourse.bass as bass
import concourse.mybir as mybir
from concourse.bass_utils import run_bass_kernel_spmd

F32 = mybir.dt.float32
BF16 = mybir.dt.bfloat16
I32 = mybir.dt.int32
AF = mybir.ActivationFunctionType
ALU = mybir.AluOpType

D = 2048
DK = 16
DFF = 5632
NFT = 44
NTOK = 1152
NX = 1024
CH = [(0, 512), (512, 512), (1024, 128)]
EPS = 1e-6
N_CORES = 8
PAIRS = [[0, 1], [2, 3], [4, 5], [6, 7]]


def rev(ap, n=None):
    pat = [list(x) for x in ap.ap]
    fs, fc = pat[-1]
    pat[-1] = [-fs, fc]
    return bass.AP(ap.tensor, ap.offset + (fc - 1) * fs, pat)


class Prog:
    ENGS = ["pe", "act", "dve", "pool", "sp"]

    def __init__(self, nc):
        self.nc = nc
        self.ops = {e: [] for e in self.ENGS}
        self.lastw = {}
        self.readers = {}
        self.dma_keys = {}
        self.final_waits = []

    def _add(self, eng, fn, r, w, dma_key):
        r = list(r) + ["__BAR__"]
        deps = set()
        for k in r:
            x = self.lastw.get(k)
            if x is not None:
                deps.add(x)
        for k in w:
            x = self.lastw.get(k)
            if x is not None:
                deps.add(x)
            for y in self.readers.get(k, ()):
                deps.add(y)
        idx = len(self.ops[eng])
        me = (eng, idx)
        deps.discard(me)
        if dma_key is not None:
            cnt = self.dma_keys.get(dma_key, 0) + 1
            self.dma_keys[dma_key] = cnt
        else:
            cnt = None
        self.ops[eng].append(dict(fn=fn, deps=deps, dma_key=dma_key, dma_cnt=cnt, sig=False))
        for k in r:
            self.readers.setdefault(k, []).append(me)
        for k in w:
            self.lastw[k] = me
            self.readers[k] = []
        return me

    def op(self, eng, fn, r=(), w=()):
        return self._add(eng, fn, r, w, None)

    def barrier(self):
        self._add("sp", lambda e: e.nop(), [], ["__BAR__"], None)

    def coll(self, fn, r, w, key):
        me = self._add("pool", fn, r, w, key)
        self.ops["pool"][me[1]]["inc"] = 1
        return me

    def dma(self, eng, fn, r=(), w=(), key=None):
        assert key is not None
        return self._add(eng, fn, r, w, key)

    def emit(self):
        nc = self.nc
        ops = self.ops
        for e in self.ENGS:
            for o in ops[e]:
                for (e2, i2) in o["deps"]:
                    o2 = ops[e2][i2]
                    if o2["dma_key"] is None:
                        if e2 == "pe" and e == "pe":
                            continue
                        o2["sig"] = True
        for e in self.ENGS:
            c = 0
            for o in ops[e]:
                if o["dma_key"] is None and o["sig"]:
                    c += 1
                    o["sigval"] = c
        import contextlib
        with contextlib.ExitStack() as st:
            csem = {e: st.enter_context(nc.semaphore("c_" + e)) for e in self.ENGS}
            dsem = {k: st.enter_context(nc.semaphore("d_%d" % i)) for i, k in enumerate(self.dma_keys)}
            block = st.enter_context(nc.Block())

            def gen(e):
                def body(eng):
                    have = {}
                    for o in ops[e]:
                        need = {}
                        for (e2, i2) in o["deps"]:
                            o2 = ops[e2][i2]
                            if o2["dma_key"] is not None:
                                kk = ("d", o2["dma_key"])
                                v = o2.get("inc", 16) * o2["dma_cnt"]
                            else:
                                if e2 == "pe" and e == "pe":
                                    continue
                                kk = ("c", e2)
                                v = o2["sigval"]
                            if need.get(kk, 0) < v:
                                need[kk] = v
                        for kk, v in need.items():
                            if have.get(kk, 0) >= v:
                                continue
                            have[kk] = v
                            sem = dsem[kk[1]] if kk[0] == "d" else csem[kk[1]]
                            eng.wait_ge(sem, v)
                        ins = o["fn"](eng)
                        if o["dma_key"] is not None:
                            if o.get("inc", 16) == 1:
                                ins.then_inc(dsem[o["dma_key"]])
                            else:
                                ins.then_inc(dsem[o["dma_key"]], 16)
                        elif o["sig"]:
                            ins.then_inc(csem[e], 1)
                    if e == "sp":
                        for k in self.final_waits:
                            eng.wait_ge(dsem[k], 16 * self.dma_keys[k])
                return body

            block.tensor(gen("pe"))
            block.scalar(gen("act"))
            block.vector(gen("dve"))
            block.gpsimd(gen("pool"))
            block.sync(gen("sp"))


def bc_last(ap, n):
    pat = [list(x) for x in ap.ap] + [[0, n]]
    return bass.AP(ap.tensor, ap.offset, pat)


def build_nc(stage=99, NG=11, dbg=False, pairs=None):
    global PAIRS
    if pairs is not None:
        PAIRS = pairs
    import contextlib
    nc = bass.Bass("TRN2", target_bir_lowering=False)
    P = Prog(nc)
    es = contextlib.ExitStack()

    def dram_in(name, shape, dt=F32):
        return nc.dram_tensor(name, list(shape), dt, kind="ExternalInput").ap()

    def dram_out(name, shape, dt=F32):
        return nc.dram_tensor(name, list(shape), dt, kind="ExternalOutput").ap()

    xT_d = dram_in("xT", [D, NTOK])
    cvec_d = dram_in("cvec", [128, 32])
    wmod_d = dram_in("w_mod", [D, 9 * D])
    bmod_d = dram_in("b_modT", [128, 144])
    ng_d = dram_in("ng", [128, 48])
    w1g_d = dram_in("w_ffn1_gate", [D, DFF])
    w1u_d = dram_in("w_ffn1_up", [D, DFF])
    w1d_d = dram_in("w_ffn1_down", [DFF, D])
    out_d = dram_out("outT", [D, NX])

    ARENA = 204800
    arena = es.enter_context(nc.sbuf_tensor("arena", [128, ARENA // 2], BF16))

    def V(off, shape, dt):
        esz = 4 if dt in (F32, I32) else 2
        n = 1
        for d_ in shape[1:]:
            n *= d_
        assert off % 4 == 0 and off + n * esz <= ARENA, (off, shape)
        ap = arena[0:shape[0], off // 2: off // 2 + n * esz // 2]
        if dt != BF16:
            ap = ap.bitcast(dt)
        if len(shape) == 3:
            ap = ap.rearrange("p (a b) -> p a b", b=shape[2])
        elif len(shape) == 4:
            ap = ap.rearrange("p (a b c) -> p a b c", b=shape[2], c=shape[3])
        return ap

    OX = 0
    OH = 73728
    OQ = 110592
    OS = 143360

    xT = V(OX, [128, DK, NTOK], F32)
    hT = V(OH, [128, DK, NTOK], BF16)

    def sb(name, shape, dt):
        return es.enter_context(nc.sbuf_tensor(name, list(shape), dt))

    modT = sb("modT", [128, 144, 2], F32)
    ng = sb("ng_sb", [128, 3, DK], F32)
    prm = sb("prm", [128, 16, DK], F32)
    onesb = sb("onesb", [128, 128], BF16)
    ident2 = sb("ident2", [2, 2], F32)
    epsb = sb("epsb", [128, 1], F32)
    cinfo = sb("cinfo_sb", [128, 4], F32)
    ps = [es.enter_context(nc.psum_tensor("ps%d" % i, [128, 512], F32)) for i in range(8)]

    A1, B1, G1, A1c, B1c, G1c, A2, B2, G2, A2c, B2c, A3, B3, G3 = range(14)

    def mm(out, lhsT, rhs, start, stop, r, w):
        P.op("pe", lambda e: e.matmul(out, lhsT=lhsT, rhs=rhs, start=start, stop=stop), r, w)

    def actv(out, in_, func, r, w, bias=None, scale=None):
        kw = {}
        if bias is not None:
            kw["bias"] = bias
        if scale is not None:
            kw["scale"] = scale
        P.op("act", lambda e: e.activation(out=out, in_=in_, func=func, **kw), r, w)

    def tt(eng, out, in0, in1, op, r, w):
        P.op(eng, lambda e: e.tensor_tensor(out=out, in0=in0, in1=in1, op=op), r, w)

    def stt(eng, out, in0, scalar, in1, op0, op1, r, w):
        P.op(eng, lambda e: e.scalar_tensor_tensor(out=out, in0=in0, scalar=scalar, in1=in1, op0=op0, op1=op1), r, w)

    def tsc(eng, out, in0, s1, s2, op0, op1, r, w):
        if s2 is None:
            P.op(eng, lambda e: e.tensor_scalar(out=out, in0=in0, scalar1=s1, scalar2=None, op0=op0), r, w)
        else:
            P.op(eng, lambda e: e.tensor_scalar(out=out, in0=in0, scalar1=s1, scalar2=s2, op0=op0, op1=op1), r, w)

    def cpy(eng, out, in_, r, w):
        if eng == "act":
            P.op(eng, lambda e: e.activation(out=out, in_=in_, func=AF.Identity), r, w)
        else:
            P.op(eng, lambda e: e.tensor_copy(out=out, in_=in_), r, w)

    def recip(out, in_, r, w):
        P.op("dve", lambda e: e.reciprocal(out=out, in_=in_), r, w)

    def mset(eng, ap, val, w):
        P.op(eng, lambda e: e.memset(ap, val), (), w)

    def asel(out, pattern, cmp, base, cm, r, w, fill=0.0):
        P.op("pool", lambda e: e.affine_select(out=out, in_=out, pattern=pattern, compare_op=cmp, fill=fill,
                                               base=base, channel_multiplier=cm), r, w)

    def dma(eng, out, in_, r, w, key):
        P.dma(eng, lambda e: e.dma_start(out=out, in_=in_), r, w, key)

    mset("pool", onesb[:, :], 1.0 / D, ["onesb"])
    mset("pool", epsb[:, :], EPS, ["epsb"])
    mset("pool", prm[:, :, :], 0.0, [("prm", i) for i in range(16)])
    mset("pool", ident2[:, :], 1.0, ["ident2"])
    asel(ident2[:, :], [[-1, 2]], ALU.is_equal, 0, 1, ["ident2"], ["ident2"])
    cinfo_d = dram_in("cinfo", [128, 4])
    dma("sp", cinfo[:, :], cinfo_d[:, :], [], ["cinfo"], "small_cinfo")

    xT_v = xT_d.rearrange("(k p) t -> p k t", p=128)
    for k4 in range(4):
        dma("sp", xT[:, 4 * k4:4 * k4 + 4, :], xT_v[:, 4 * k4:4 * k4 + 4, :], [],
            [("xT", k, c) for k in range(4 * k4, 4 * k4 + 4) for c in range(3)], ("xload", k4))
    dma("sp", ng[:, :, :].rearrange("p a b -> p (a b)"), ng_d[:, :], [], ["ng"], "small_ng")

    cv = V(OQ, [128, 32], F32)
    csil = sb("csil_sb", [128, 32], BF16)
    bmT = sb("bmT_sb", [128, 144], F32)
    wmb = [V(OQ + 1024 + i * 16384, [128, DK, 512], BF16) for i in range(2)]
    mrow = [V(OQ + 1024 + 32768 + i * 2048, [2, 512], F32) for i in range(2)]
    dma("sp", cv[:, :], cvec_d[:, :], [], ["cv"], "small_cv")
    dma("sp", bmT[:, :], bmod_d[:, :], [], ["bmT"], "small_bm")
    actv(csil[:, :], cv[:, :], AF.Silu, ["cv"], ["csil"])
    wmod_v = wmod_d.rearrange("(k p) f -> p k f", p=128)
    modT_ps = ps[7]
    for cb in range(12):
        wb = wmb[cb % 2]
        dma("pool", wb[:, :, :], wmod_v[:, :, cb * 512:(cb + 1) * 512], [], [("wmb", cb % 2)], ("wmb", cb % 2))
        pst = ps[cb % 2]
        for k in range(DK):
            mm(pst[0:2, :], csil[:, 2 * k:2 * k + 2], wb[:, k, :], k == 0, k == DK - 1,
               ["csil", ("wmb", cb % 2)], [("ps", cb % 2)])
        mr = mrow[cb % 2]
        cpy("dve", mr[:, :], pst[0:2, :], [("ps", cb % 2)], [("mrow", cb % 2)])
        for q in range(4):
            t = cb * 4 + q
            mm(modT_ps[:, 2 * t:2 * t + 2], mr[:, q * 128:(q + 1) * 128], ident2[:, :], True, True,
               [("mrow", cb % 2), "ident2"], ["modT_ps"])
    mps_v = modT_ps[:, 0:288].rearrange("p (a b) -> p a b", b=2)
    for col in range(2):
        tt("dve", modT[:, 0:48, col], mps_v[:, 0:48, col], bmT[:, 0:48], ALU.add, ["modT_ps", "bmT"], ["modT"])
    P.barrier()

    wmL = [V(182272 + i * 8192, [128, DK, 256], BF16) for i in range(2)]
    mrL = V(198656, [2, 256], F32)
    late = dict(nxt=0, pend=None)

    def late_mod_step(issue=True):
        b = late["pend"]
        if b is not None:
            wb_ = wmL[b % 2]
            for k in range(DK):
                mm(ps[6][0:2, 0:256], csil[:, 2 * k:2 * k + 2], wb_[:, k, :], k == 0, k == DK - 1,
                   ["csil", ("wmL", b % 2)], [("ps", 6)])
            cpy("dve", mrL[:, :], ps[6][0:2, 0:256], [("ps", 6)], ["mrL"])
            for q in range(2):
                t = 48 + 2 * b + q
                mm(modT_ps[:, 2 * t:2 * t + 2], mrL[:, q * 128:(q + 1) * 128], ident2[:, :], True, True,
                   ["mrL", "ident2"], ["modT_ps"])
            late["pend"] = None
        if issue and late["nxt"] < 48:
            b = late["nxt"]
            late["nxt"] += 1
            c_ = 6144 + 256 * b
            dma("pool", wmL[b % 2][:, :, :], wmod_v[:, :, c_:c_ + 256], [], [("wmL", b % 2)], ("wmL", b % 2))
            late["pend"] = b

    def mv(n, col):
        return modT[:, n * 16:(n + 1) * 16, col]

    def derive_A(slot, gi, n, col):
        stt("dve", prm[:, slot, :], mv(n, col), 1.0, ng[:, gi, :], ALU.add, ALU.mult, ["modT", "ng"], [("prm", slot)])

    def derive_copy(slot, n, col, mul=1.0):
        tsc("dve", prm[:, slot, :], mv(n, col), mul, None, ALU.mult, None, ["modT"], [("prm", slot)])

    derive_A(A1, 0, 1, 0); derive_copy(B1, 0, 0); derive_copy(G1, 2, 0, 0.5)
    derive_A(A1c, 0, 1, 1); derive_copy(B1c, 0, 1); derive_copy(G1c, 2, 1, 0.5)

    def norm_mod(tag, xv, hv, Ax, Bx, Ac, Bc, chunks):
        sq = [V(OQ + i * 1024, [128, 512], BF16) for i in range(3)]
        rstd = V(OQ + 3072, [128, 512], F32)
        tmp = [V(OQ + 5120 + i * 2048, [128, 512], F32) for i in range(2)]
        for ci in chunks:
            c0, cl = CH[ci]
            pst = ps[6]
            for k in range(DK):
                s_ = sq[k % 3]
                actv(s_[:, 0:cl], xv[:, k, c0:c0 + cl], AF.Square, [("xT", k, ci)], [("sq", tag, k % 3)])
                mm(pst[:, 0:cl], onesb[:, :], s_[:, 0:cl], k == 0, k == DK - 1, [("sq", tag, k % 3), "onesb"], [("ps", 6)])
            actv(rstd[:, 0:cl], pst[:, 0:cl], AF.Sqrt, [("ps", 6), "epsb"], [("rstd", tag)], bias=epsb[:, 0:1], scale=1.0)
            recip(rstd[:, 0:cl], rstd[:, 0:cl], [("rstd", tag)], [("rstd", tag)])
            a_s, b_s = (Ac, Bc) if ci == 2 else (Ax, Bx)
            for k in range(DK):
                t = tmp[k % 2]
                tt("dve", t[:, 0:cl], xv[:, k, c0:c0 + cl], rstd[:, 0:cl], ALU.mult,
                   [("xT", k, ci), ("rstd", tag)], [("ntmp", tag, k % 2)])
                actv(hv[:, k, c0:c0 + cl], t[:, 0:cl], AF.Identity,
                     [("ntmp", tag, k % 2), ("prm", a_s), ("prm", b_s)], [("hT", k, ci)],
                     bias=prm[:, b_s, k:k + 1], scale=prm[:, a_s, k:k + 1])

    def ffn(tag, xv, hv, ntok, wg_d, wu_d, wd_d, Gx, Gc, chunks, hook=None):
        wg_v = wg_d.rearrange("(k p) f -> p k f", p=128)
        wu_v = wu_d.rearrange("(k p) f -> p k f", p=128)
        wd_v = wd_d.rearrange("(k p) d -> p k d", p=128)
        o = OQ
        wgb = [V(o + i * 8192, [128, DK, 256], BF16) for i in range(2)]
        wub = [V(o + 16384 + i * 8192, [128, DK, 256], BF16) for i in range(2)]
        wdb = [V(o + 32768 + i * 8192, [128, 4, 1024], BF16) for i in range(2)]
        act = [V(o + 49152 + i * 9216, [128, 4, ntok], BF16) for i in range(2)]
        stmp = [V(o + 67584 + i * 2048, [128, 512], F32) for i in range(2)]
        cnt = dict(nb=0, nd=0, ne=0)

        def gu_group(g):
            a = act[g % 2]
            for half in range(2):
                slot = cnt["nb"] % 2
                cnt["nb"] += 1
                f0 = g * 512 + half * 256
                dma("pool", wgb[slot][:, :, :], wg_v[:, :, f0:f0 + 256], [], [("wgb", tag, slot)], ("wgb", tag, slot))
                dma("pool", wub[slot][:, :, :], wu_v[:, :, f0:f0 + 256], [], [("wub", tag, slot)], ("wub", tag, slot))
                for fl in range(2):
                    fi = half * 2 + fl
                    if hook is not None:
                        hook()
                    for ci in chunks:
                        c0, cl = CH[ci]
                        ne = cnt["ne"]
                        cnt["ne"] += 1
                        bg = (ne % 2) * 2
                        pg, pu = ps[bg], ps[bg + 1]
                        for k in range(DK):
                            mm(pg[:, 0:cl], wgb[slot][:, k, fl * 128:(fl + 1) * 128], hv[:, k, c0:c0 + cl], k == 0, k == DK - 1,
                               [("wgb", tag, slot), ("hT", k, ci)], [("ps", bg)])
                        for k in range(DK):
                            mm(pu[:, 0:cl], wub[slot][:, k, fl * 128:(fl + 1) * 128], hv[:, k, c0:c0 + cl], k == 0, k == DK - 1,
                               [("wub", tag, slot), ("hT", k, ci)], [("ps", bg + 1)])
                        st_ = stmp[ne % 2]
                        actv(st_[:, 0:cl], pg[:, 0:cl], AF.Silu, [("ps", bg)], [("stmp", tag, ne % 2)])
                        tt("dve", a[:, fi, c0:c0 + cl], st_[:, 0:cl], pu[:, 0:cl], ALU.mult,
                           [("stmp", tag, ne % 2), ("ps", bg + 1)], [("act", tag, g % 2, fi, ci)])

        def down_group(g):
            a = act[g % 2]
            for dh in range(2):
                slot = cnt["nd"] % 2
                cnt["nd"] += 1
                dma("pool", wdb[slot][:, :, :], wd_v[:, 4 * g:4 * g + 4, dh * 1024:(dh + 1) * 1024], [],
                    [("wdb", tag, slot)], ("wdb", tag, slot))
                for dl in range(8):
                    dk = dh * 8 + dl
                    for ci in chunks:
                        c0, cl = CH[ci]
                        bank = 4 + ((dk * 3 + ci) % 2)
                        pd = ps[bank]
                        for fi in range(4):
                            mm(pd[:, 0:cl], wdb[slot][:, fi, dl * 128:(dl + 1) * 128], a[:, fi, c0:c0 + cl], fi == 0, fi == 3,
                               [("wdb", tag, slot), ("act", tag, g % 2, fi, ci)], [("ps", bank)])
                        gs = Gc if ci == 2 else Gx
                        stt("dve", xv[:, dk, c0:c0 + cl], pd[:, 0:cl], prm[:, gs, dk:dk + 1], xv[:, dk, c0:c0 + cl],
                            ALU.mult, ALU.add, [("ps", bank), ("prm", gs), ("xT", dk, ci)], [("xT", dk, ci)])

        gu_group(0)
        for g in range(1, NG):
            gu_group(g)
            down_group(g - 1)
        down_group(NG - 1)

    norm_mod("n1", xT, hT, A1, B1, A1c, B1c, [0, 1, 2])
    P.barrier()
    ffn("f1", xT, hT, NTOK, w1g_d, w1u_d, w1d_d, G1, G1c, [0, 1, 2], hook=late_mod_step)
    while late["pend"] is not None or late["nxt"] < 48:
        late_mod_step()
    for col in range(2):
        tt("dve", modT[:, 48:144, col], mps_v[:, 48:144, col], bmT[:, 48:144], ALU.add, ["modT_ps", "bmT"], ["modT"])
    derive_A(A2, 1, 4, 0); derive_copy(B2, 3, 0); derive_copy(G2, 5, 0)
    derive_A(A2c, 1, 4, 1); derive_copy(B2c, 3, 1)
    derive_A(A3, 2, 7, 0); derive_copy(B3, 6, 0); derive_copy(G3, 8, 0, 0.5)
    P.barrier()
    final_x = xT

    if dbg and stage == 1:
        d_mod = dram_out("d_mod", [128, 288])
        d_hT = dram_out("d_hT", [128, DK * NTOK], BF16)
        dma("sp", d_mod[:, :], modT[:, :, :].rearrange("p a b -> p (a b)"), ["modT"], [], "dbg1")
        dma("sp", d_hT[:, :], hT[:, :, :].rearrange("p a b -> p (a b)"), [("hT", k, c) for k in range(DK) for c in range(3)], [], "dbg3")
        P.final_waits += ["dbg1", "dbg3"]

    if stage >= 2:
        HD = 128
        win_d = dram_in("w_in", [D, 8192])
        qkg_d = dram_in("qkg", [128, 2])
        sK = nc.dram_tensor("sK", [576, 1024], BF16); rK = nc.dram_tensor("rK", [1152, 1024], BF16)
        sV = nc.dram_tensor("sV", [576, 1024], BF16); rV = nc.dram_tensor("rV", [1152, 1024], BF16)
        sU = [nc.dram_tensor("sU%d" % i, [576, 1024], BF16) for i in range(2)]
        rU = [nc.dram_tensor("rU%d" % i, [1152, 1024], BF16) for i in range(2)]
        x1s = nc.dram_tensor("x1s", [128, DK * NX], F32)
        h2s = nc.dram_tensor("h2s", [128, DK * NX], BF16)
        win_v = win_d.rearrange("(k p) f -> p k f", p=128)

        qT = V(OQ, [128, 16, NX], BF16)
        qkg = sb("qkg_sb", [128, 3], F32)
        ones1 = sb("ones1", [128, 128], BF16)
        onesh = sb("onesh", [128, 128], BF16)
        Pm = sb("Pm", [128, 128], BF16)
        dma("sp", qkg[:, 0:2], qkg_d[:, :], [], ["qkg"], "small_qkg")
        tsc("dve", qkg[:, 2:3], qkg[:, 0:1], HD ** -0.5, None, ALU.mult, None, ["qkg"], ["qkg2"])
        mset("pool", ones1[:, :], 1.0, ["ones1"])
        mset("pool", onesh[:, :], 1.0 / HD, ["onesh"])

        norm_mod("n2", xT, hT, A2, B2, A2c, B2c, [0, 1, 2])
        x1s_v = x1s.ap().rearrange("p (k t) -> p k t", t=NX)
        for k4 in range(4):
            dma("sp", x1s_v[:, 4 * k4:4 * k4 + 4, :], xT[:, 4 * k4:4 * k4 + 4, 0:NX],
                [("xT", k, c) for k in range(4 * k4, 4 * k4 + 4) for c in range(3)], [("x1s", k4)], ("x1spill", k4))
        P.barrier()

        cosT = V(OX, [128, NX], F32)
        sinT = V(OX + 4096, [128, NX], F32)
        o = OX + 8192
        wblk = [V(o + i * 16384, [128, DK, 512], BF16) for i in range(2)]
        o += 32768
        pid_i = V(o, [128, 1], I32)
        pidf = V(o + 64, [128, 1], F32)
        invf = V(o + 128, [128, 1], F32)
        pos_i = V(o + 256, [128, NX], I32)
        posf = V(o + 256 + 4096, [128, NX], F32)
        turn = V(o + 256 + 8192, [128, NX], F32)
        ki = V(o + 256 + 12288, [128, NX], I32)
        pa = V(o + 256 + 16384, [128, 128], F32)
        pb_ = V(o + 256 + 16384 + 512, [128, 128], F32)
        P.op("pool", lambda e: e.iota(pid_i[:, :], pattern=[[0, 1]], base=0, channel_multiplier=1), [], ["pid_i"])
        P.op("dve", lambda e: e.tensor_single_scalar(out=pid_i[:, :], in_=pid_i[:, :], scalar=31, op=ALU.bitwise_and), ["pid_i"], ["pid_i"])
        cpy("pool", pidf[:, :], pid_i[:, :], ["pid_i"], ["pidf"])
        actv(invf[:, :], pidf[:, :], AF.Exp, ["pidf"], ["invf"], scale=-math.log(10000.0) / 32.0)
        P.op("pool", lambda e: e.iota(pos_i[0:64, :].rearrange("p (a b) -> p a b", b=64), pattern=[[1, 16], [0, 64]], base=0, channel_multiplier=0), [], ["pos_i0"])
        P.op("pool", lambda e: e.iota(pos_i[64:128, :].rearrange("p (a b) -> p a b", b=64), pattern=[[0, 16], [1, 64]], base=0, channel_multiplier=0), [], ["pos_i1"])
        cpy("pool", posf[:, :], pos_i[:, :], ["pos_i0", "pos_i1"], ["posf"])
        tsc("pool", posf[0:64, :], posf[0:64, :], cinfo[0:64, 2:3], None, ALU.add, None, ["posf", "cinfo"], ["posf"])
        tsc("dve", turn[:, :], posf[:, :], invf[:, 0:1], 1.0 / (2 * math.pi), ALU.mult, ALU.mult, ["posf", "invf"], ["turn"])
        cpy("dve", ki[:, :], turn[:, :], ["turn"], ["ki"])
        tt("dve", posf[:, :], turn[:, :], ki[:, :], ALU.subtract, ["turn", "ki"], ["posf"])
        actv(sinT[:, :], posf[:, :], AF.Sin, ["posf"], ["sinT"], scale=2 * math.pi)
        tsc("dve", turn[:, :], turn[:, :], 0.25, None, ALU.add, None, ["turn"], ["turn"])
        cpy("dve", ki[:, :], turn[:, :], ["turn"], ["ki"])
        tt("dve", posf[:, :], turn[:, :], ki[:, :], ALU.subtract, ["turn", "ki", "sinT"], ["posf"])
        actv(cosT[:, :], posf[:, :], AF.Sin, ["posf"], ["cosT"], scale=2 * math.pi)
        mset("pool", pa[:, :], -1.0, ["pa"])
        asel(pa[:, :], [[-1, 128]], ALU.is_equal, -32, 1, ["pa"], ["pa"])
        mset("pool", pa[:, 32:64], 0.0, ["pa"])
        mset("pool", pa[:, 96:128], 0.0, ["pa"])
        mset("pool", pb_[:, :], 1.0, ["pb_"])
        asel(pb_[:, :], [[-1, 128]], ALU.is_equal, 32, 1, ["pb_"], ["pb_"])
        mset("pool", pb_[:, 0:32], 0.0, ["pb_"])
        mset("pool", pb_[:, 64:96], 0.0, ["pb_"])
        tt("pool", Pm[:, :], pa[:, :], pb_[:, :], ALU.add, ["pa", "pb_"], ["Pm"])
        P.barrier()

        kst = [V(o + i * 2304, [128, NTOK], BF16) for i in range(2)]
        vst = [V(o + 4608 + i * 1024, [128, 512], BF16) for i in range(2)]
        ust = [V(o + 6656 + i * 2304, [128, NTOK], BF16) for i in range(2)]
        sqb = [V(o + 11264 + i * 1024, [128, 512], BF16) for i in range(3)]
        rsb = [V(o + 14336 + i * 2048, [128, 512], F32) for i in range(3)]
        knb = [V(o + 20480 + i * 1024, [128, 512], BF16) for i in range(3)]
        t1b = [V(o + 23552 + i * 2048, [128, 512], F32) for i in range(2)]
        t2b = [V(o + 27648 + i * 2048, [128, 512], F32) for i in range(2)]
        sb_k = sK.ap().rearrange("(h r) c -> h (r c)", h=4).rearrange("h (d t) -> h d t", d=128)
        sb_v = sV.ap().rearrange("(t r) c -> t (r c)", t=9).rearrange("t (p c) -> t p c", p=128)
        sb_u = [sU[i].ap().rearrange("(a r) c -> a (r c)", a=4).rearrange("a (p t) -> a p t", p=128) for i in range(2)]

        def allgather(src, dst, rkeys, wkey, key):
            P.coll(lambda e: e.collective_compute("AllGather", ALU.bypass, replica_groups=PAIRS,
                                                  ins=[src.ap().opt()], outs=[dst.ap().opt()]), rkeys, [wkey], key)

        cnt2 = dict(w=0, n=0, b=0)

        def load_w(col0):
            slot = cnt2["w"] % 2
            cnt2["w"] += 1
            dma("pool", wblk[slot][:, :, :], win_v[:, :, col0:col0 + 512], [], [("wblk", slot)], ("wblk", slot))
            return slot

        pipe = []

        def qk_unit(slot, ctl, ci, gain, dest, dkey, rope, post=None):
            c0, cl = CH[ci]
            n = cnt2["n"]
            cnt2["n"] += 1
            i3, i2 = n % 3, n % 2
            pb, pkey = ps[i3], ("ps", i3)

            def stA():
                for k in range(DK):
                    mm(pb[:, 0:cl], wblk[slot][:, k, ctl * 128:(ctl + 1) * 128], hT[:, k, c0:c0 + cl], k == 0, k == DK - 1,
                       [("wblk", slot), ("hT", k, ci)], [pkey])
                actv(sqb[i3][:, 0:cl], pb[:, 0:cl], AF.Square, [pkey], [("sqb", i3)])

            def stB():
                mm(ps[3 + i2][:, 0:cl], onesh[:, :], sqb[i3][:, 0:cl], True, True, [("sqb", i3), "onesh"], [("ps", 3 + i2)])
                actv(rsb[i3][:, 0:cl], ps[3 + i2][:, 0:cl], AF.Sqrt, [("ps", 3 + i2), "epsb"], [("rsb", i3)], bias=epsb[:, 0:1], scale=1.0)
                recip(rsb[i3][:, 0:cl], rsb[i3][:, 0:cl], [("rsb", i3)], [("rsb", i3)])
                if not rope:
                    stt("dve", dest, pb[:, 0:cl], gain, rsb[i3][:, 0:cl], ALU.mult, ALU.mult, [pkey, ("rsb", i3), "qkg", "qkg2"], [dkey])
                else:
                    stt("dve", knb[i3][:, 0:cl], pb[:, 0:cl], gain, rsb[i3][:, 0:cl], ALU.mult, ALU.mult, [pkey, ("rsb", i3), "qkg", "qkg2"], [("knb", i3)])

            def stC():
                if rope:
                    mm(ps[5 + i2][:, 0:cl], Pm[:, :], knb[i3][:, 0:cl], True, True, [("knb", i3), "Pm"], [("ps", 5 + i2)])
                    tt("dve", t1b[i2][:, 0:cl], knb[i3][:, 0:cl], cosT[:, c0:c0 + cl], ALU.mult, [("knb", i3), "cosT"], [("t1b", i2)])
                    tt("dve", t2b[i2][:, 0:cl], ps[5 + i2][:, 0:cl], sinT[:, c0:c0 + cl], ALU.mult, [("ps", 5 + i2), "sinT"], [("t2b", i2)])
                    tt("dve", dest, t1b[i2][:, 0:cl], t2b[i2][:, 0:cl], ALU.add, [("t1b", i2), ("t2b", i2)], [dkey])
                if post is not None:
                    post()
            pipe.append([stA, stB, stC])
            advance()

        def advance(flush=False):
            if not flush:
                pipe[-1][0]()
                if len(pipe) >= 2:
                    pipe[-2][1]()
                if len(pipe) >= 3:
                    pipe[-3][2]()
            else:
                if len(pipe) >= 1:
                    pipe[-1][1]()
                if len(pipe) >= 2:
                    pipe[-2][2]()
                if len(pipe) >= 1:
                    pipe[-1][2]()
                del pipe[:]

        def proj_fm(slot, ctl, ci):
            b = cnt2["b"] % 2
            cnt2["b"] += 1
            c0, cl = CH[ci]
            for k in range(DK):
                mm(ps[b][:, 0:cl], wblk[slot][:, k, ctl * 128:(ctl + 1) * 128], hT[:, k, c0:c0 + cl], k == 0, k == DK - 1,
                   [("wblk", slot), ("hT", k, ci)], [("ps", b)])
            return b

        slot = load_w(0)
        for h in range(4):
            for ci in range(3):
                c0, cl = CH[ci]
                post = None
                if ci == 2:
                    def post(h=h):
                        dma("sp", sb_k[h], kst[h % 2][:, :], [("kst", h % 2, c_) for c_ in range(3)], [("sendb", "k", h)], ("sendk", h % 2))
                qk_unit(slot, h, ci, qkg[:, 1:2], kst[h % 2][:, c0:c0 + cl], ("kst", h % 2, ci), ci < 2, post)
        advance(flush=True)
        allgather(sK, rK, [("sendb", "k", h) for h in range(4)], "rK", "cc1k")
        slot = load_w(512)
        for t9 in range(9):
            ci = 0 if t9 < 4 else (1 if t9 < 8 else 2)
            b = 7
            for k in range(DK):
                mm(ps[b][:, :], hT[:, k, t9 * 128:(t9 + 1) * 128], wblk[slot][:, k, :], k == 0, k == DK - 1,
                   [("wblk", slot), ("hT", k, ci)], [("ps", b)])
            cpy("act", vst[t9 % 2][:, :], ps[b][:, :], [("ps", b)], [("vst", t9 % 2)])
            dma("sp", sb_v[t9], vst[t9 % 2][:, :], [("vst", t9 % 2)], [("sendb", "v", t9)], ("sendv", t9 % 2))
        allgather(sV, rV, [("sendb", "v", t9) for t9 in range(9)], "rV", "cc1v")
        for ub in range(2):
            slot = load_w(1024 + ub * 512)
            for ctl in range(4):
                for ci in range(3):
                    c0, cl = CH[ci]
                    b = proj_fm(slot, ctl, ci)
                    cpy("act", ust[ctl % 2][:, c0:c0 + cl], ps[b][:, 0:cl], [("ps", b)], [("ust", ctl % 2, ci)])
                dma("sp", sb_u[ub][ctl], ust[ctl % 2][:, :], [("ust", ctl % 2, ci) for ci in range(3)], [("sendb", "u", ub * 4 + ctl)], ("sendu", ctl % 2))
            allgather(sU[ub], rU[ub], [("sendb", "u", ub * 4 + ctl) for ctl in range(4)], ("rU", ub), "cc1u%d" % ub)
        for qb in range(4):
            slot = load_w(2048 + qb * 512)
            for hl in range(4):
                h = qb * 4 + hl
                for ci in range(2):
                    c0, cl = CH[ci]
                    qk_unit(slot, hl, ci, qkg[:, 2:3], qT[:, h, c0:c0 + cl], ("qT", h, ci), True)
        advance(flush=True)
        h2s_v = h2s.ap().rearrange("p (k t) -> p k t", t=NX)
        dma("sp", h2s_v, hT[:, :, 0:NX], [("hT", k, c) for k in range(DK) for c in range(3)], ["h2s"], "h2spill")
        P.barrier()

        KT = V(OX, [128, 4, 2304], BF16)
        Vf = V(OX + 18432, [128, 18, 512], BF16)
        pT = [V(OX + 36864 + i * 1024, [128, 512], BF16) for i in range(3)]
        rden = [V(OX + 39936 + i * 2048, [128, 512], F32) for i in range(2)]
        for r_ in range(2):
            rk = rK[r_ * 576:(r_ + 1) * 576, :].rearrange("(h r) c -> h (r c)", h=4).rearrange("h (d t) -> h d t", d=128)
            for h in range(4):
                dma("sp", KT[:, h, r_ * NTOK:(r_ + 1) * NTOK], rk[h], ["rK"], [("KT", h)], ("ldk", r_, h))
            rv = rV[r_ * 576:(r_ + 1) * 576, :].rearrange("(t r) c -> t (r c)", t=9).rearrange("t (p c) -> p t c", p=128)
            dma("sp", Vf[:, r_ * 9:(r_ + 1) * 9, :], rv, ["rV"], ["Vf"], ("ldv", r_))
        na = 0
        for h in range(16):
            kvh = h // 4
            for qc in range(2):
                ob = 2 + (na % 2)
                db = 4 + (na % 2)
                q_ap = qT[:, h, qc * 512:(qc + 1) * 512]

                def smm(kt, h=h, kvh=kvh, qc=qc, q_ap=q_ap):
                    sbk = kt % 2
                    mm(ps[sbk][:, :], KT[:, kvh, kt * 128:(kt + 1) * 128], q_ap, True, True,
                       [("KT", kvh), ("qT", h, qc)], [("ps", sbk)])
                smm(0)
                for kt in range(18):
                    if kt + 1 < 18:
                        smm(kt + 1)
                    sbk = kt % 2
                    pt = pT[kt % 3]
                    actv(pt[:, :], ps[sbk][:, :], AF.Exp, [("ps", sbk)], [("pT", kt % 3)])
                    mm(ps[ob][:, :], Vf[:, kt, kvh * 128:(kvh + 1) * 128], pt[:, :], kt == 0, kt == 17, ["Vf", ("pT", kt % 3)], [("ps", ob)])
                    mm(ps[db][:, :], ones1[:, :], pt[:, :], kt == 0, kt == 17, ["ones1", ("pT", kt % 3)], [("ps", db)])
                rd = rden[na % 2]
                recip(rd[:, :], ps[db][:, :], [("ps", db)], [("rden", na % 2)])
                tt("dve", q_ap, ps[ob][:, :], rd[:, :], ALU.mult, [("ps", ob), ("rden", na % 2)], [("qT", h, qc)])
                na += 1
        P.barrier()
        if dbg and stage == 2:
            d_at = dram_out("d_attn", [128, 16 * NX], BF16)
            dma("sp", d_at[:, :], qT[:, :, :].rearrange("p a b -> p (a b)"), [("qT", h, c) for h in range(16) for c in range(2)], [], "dbg4")
            P.final_waits += ["dbg4"]

    if stage >= 3:
        LW = 2304
        ssmA_d = dram_in("ssmA", [128, 192])
        ssmB_d = dram_in("ssmB", [128, 2048])
        ssmC_d = dram_in("ssmC", [128, 2048])
        ssmD_d = dram_in("ssmD", [128, 4])
        send2 = nc.dram_tensor("send2", [1024, 1024], BF16)
        recv2 = nc.dram_tensor("recv2", [2048, 1024], BF16)
        identF = sb("identF", [128, 128], F32)
        identB = sb("identB", [128, 128], BF16)
        mask8 = sb("mask8", [128, 8], F32)
        ssmD = sb("ssmD_sb", [128, 4], F32)
        rdec = sb("rdec", [128, 64], F32)
        phi = sb("phi", [128, 64], F32)
        psi = sb("psi", [128, 64], F32)
        q25 = sb("q25", [128, 1], F32)
        mset("pool", q25[:, :], 1.0, ["q25"])
        mset("pool", identF[:, :], 1.0, ["identF"])
        asel(identF[:, :], [[-1, 128]], ALU.is_equal, 0, 1, ["identF"], ["identF"])
        cpy("pool", identB[:, :], identF[:, :], ["identF"], ["identB"])
        mset("pool", mask8[:, :], 1.0, ["mask8"])
        asel(mask8[:, :], [[-16, 8]], ALU.is_ge, 0, 1, ["mask8"], ["mask8"])
        asel(mask8[:, :], [[16, 8]], ALU.is_ge, 15, -1, ["mask8"], ["mask8"])
        dma("sp", ssmD[:, :], ssmD_d[:, :], [], ["ssmD"], "small_ssmD")

        uTf = V(OS, [128, 4, LW], BF16)
        Atab = V(OS + 18432, [128, LW], BF16)
        Btab = V(OS + 23040, [128, LW], BF16)
        Gm = V(OS + 27648, [128, 64, 16], F32)
        Gp = V(OS + 31744, [128, 64, 16], F32)
        Em = V(OS + 35840, [128, 64, 16], BF16)
        Ep = V(OS + 37888, [128, 64, 16], BF16)
        BTp = V(OS + 39936, [128, 8, 128], BF16)
        BTq = V(OS + 41984, [128, 8, 128], BF16)
        ETp = V(OS + 44032, [128, 8, 128], BF16)
        ETq = V(OS + 46080, [128, 8, 128], BF16)
        M1 = V(OS + 48128, [128, 2048], BF16)
        M2 = V(OS + 52224, [128, 2048], BF16)
        Ddiag = V(OS + 56320, [128, 4, 128], BF16)

        o = OX
        sA = V(o, [128, 3, 64], F32); o += 768
        sB = V(o, [128, 2, 64, 16], F32); o += 8192
        sC = V(o, [128, 2, 64, 16], F32); o += 8192
        T = [V(o + i * 4096, [128, 64, 16], F32) for i in range(4)]; o += 16384
        sm = [V(o + i * 256, [128, 64], F32) for i in range(16)]; o += 4096
        smi = V(o, [128, 64], I32); o += 256
        tabi = V(o, [128, LW], I32); o += 9216
        dma("sp", sA[:, :, :].rearrange("p a b -> p (a b)"), ssmA_d[:, :], [], ["sA"], "small_sA")
        dma("sp", sB[:, :, :, :].rearrange("p a b c -> p (a b c)"), ssmB_d[:, :], [], ["sB"], "small_sB")
        dma("sp", sC[:, :, :, :].rearrange("p a b c -> p (a b c)"), ssmC_d[:, :], [], ["sC"], "small_sC")
        are, aim = sA[:, 0, :], sA[:, 1, :]
        dt_, adt, th, sin1, cos1, lbr, lbi, lm1, den, cre, cim, tA, tB, tC = sm[0:14]
        K_ = ["ssmset"]
        actv(dt_, sA[:, 2, :], AF.Exp, ["sA"], K_)
        tt("dve", adt, are, dt_, ALU.mult, K_ + ["sA"], K_)
        tt("dve", th, aim, dt_, ALU.mult, K_, K_)
        actv(rdec[:, :], adt, AF.Exp, K_, ["rdec"])
        tsc("dve", phi[:, :], th, 1.0 / (2 * math.pi), None, ALU.mult, None, K_, ["phi"])
        tsc("dve", tA, phi[:, :], 64.0, None, ALU.mult, None, ["phi"], K_)
        cpy("dve", smi, tA, K_, K_)
        tt("dve", psi[:, :], tA, smi, ALU.subtract, K_, ["psi"])
        cpy("dve", smi, phi[:, :], K_ + ["phi", "psi"], K_)
        tt("dve", tB, phi[:, :], smi, ALU.subtract, K_, K_)
        actv(sin1, tB, AF.Sin, K_, K_, scale=2 * math.pi)
        tsc("dve", tC, phi[:, :], 0.25, None, ALU.add, None, K_, K_)
        cpy("dve", smi, tC, K_, K_)
        tt("dve", tB, tC, smi, ALU.subtract, K_, K_)
        actv(cos1, tB, AF.Sin, K_, K_, scale=2 * math.pi)
        tt("dve", lbr, rdec[:, :], cos1, ALU.mult, K_ + ["rdec"], K_)
        tt("dve", lbi, rdec[:, :], sin1, ALU.mult, K_, K_)
        tsc("dve", lm1, lbr, -1.0, None, ALU.add, None, K_, K_)
        tt("dve", den, are, are, ALU.mult, K_, K_)
        tt("dve", tA, aim, aim, ALU.mult, K_, K_)
        tt("dve", den, den, tA, ALU.add, K_, K_)
        recip(den, den, K_, K_)
        tt("dve", tA, lm1, are, ALU.mult, K_, K_)
        tt("dve", tB, lbi, aim, ALU.mult, K_, K_)
        tt("dve", tA, tA, tB, ALU.add, K_, K_)
        tt("dve", cre, tA, den, ALU.mult, K_, K_)
        tt("dve", tA, lbi, are, ALU.mult, K_, K_)
        tt("dve", tB, lm1, aim, ALU.mult, K_, K_)
        tt("dve", tA, tA, tB, ALU.subtract, K_, K_)
        tt("dve", cim, tA, den, ALU.mult, K_, K_)
        Br, Bi = sB[:, 0, :, :], sB[:, 1, :, :]
        Cr, Ci = sC[:, 0, :, :], sC[:, 1, :, :]
        tt("pool", T[0][:, :, :], Br, bc_last(cre, 16), ALU.mult, K_ + ["sB"], ["T0"])
        tt("pool", T[1][:, :, :], Bi, bc_last(cim, 16), ALU.mult, K_ + ["sB"], ["T1"])
        tt("pool", T[2][:, :, :], Bi, bc_last(cre, 16), ALU.mult, K_ + ["sB"], ["T2"])
        tt("pool", T[3][:, :, :], Br, bc_last(cim, 16), ALU.mult, K_ + ["sB"], ["T3"])
        tt("dve", Gm[0:64], T[0][0:64], T[1][0:64], ALU.subtract, ["T0", "T1"], ["Gm0"])
        tt("dve", Gm[64:128], T[2][64:128], T[3][64:128], ALU.add, ["T2", "T3"], ["Gm1"])
        tt("dve", Gp[0:64], T[2][0:64], T[3][0:64], ALU.add, ["T2", "T3"], ["Gp0"])
        tt("dve", Gp[64:128], T[1][64:128], T[0][64:128], ALU.subtract, ["T0", "T1"], ["Gp1"])
        cpy("pool", Em[0:64], Cr[0:64], ["sC"], ["Em0"])
        tsc("pool", Em[64:128], Ci[64:128], -1.0, None, ALU.mult, None, ["sC"], ["Em1"])
        tsc("pool", Ep[0:64], Ci[0:64], -1.0, None, ALU.mult, None, ["sC"], ["Ep0"])
        tsc("pool", Ep[64:128], Cr[64:128], -1.0, None, ALU.mult, None, ["sC"], ["Ep1"])
        P.op("pool", lambda e: e.iota(tabi[:, :].rearrange("p (a b) -> p a b", b=64), pattern=[[1, 36], [0, 64]], base=0, channel_multiplier=0), [], ["tabi"])
        cpy("pool", Atab[:, :], tabi[:, :], ["tabi"], ["Atab"])
        P.op("pool", lambda e: e.iota(tabi[:, :].rearrange("p (a b) -> p a b", b=64), pattern=[[0, 36], [1, 64]], base=0, channel_multiplier=0), ["Atab"], ["tabi"])
        cpy("pool", Btab[:, :], tabi[:, :], ["tabi"], ["Btab"])
        for ct in range(4):
            tsc("dve", Ddiag[:, ct, :], identF[:, :], ssmD[:, ct:ct + 1], None, ALU.mult, None, ["identF", "ssmD"], [("Ddiag", ct)])
        P.barrier()

        stA = V(OX, [128, NTOK], BF16)
        stB = V(OX + 2304, [128, NTOK], BF16)
        for r_ in range(2):
            ru = [rU[ub][r_ * 576:(r_ + 1) * 576, :].rearrange("(a r) c -> a (r c)", a=4).rearrange("a (p t) -> a p t", p=128) for ub in range(2)]
            for i in range(4):
                dma("sp", stA[:, :], ru[0][i], [("rU", 0)], ["stA"], "ldsta")
                dma("sp", stB[:, :], ru[1][i], [("rU", 1)], ["stB"], "ldstb")
                tsc("dve", stA[:, :], stA[:, :], cinfo[:, 1:2], None, ALU.mult, None, ["stA", "cinfo"], ["stA"])
                stt("dve", uTf[:, i, 256 + r_ * 1024:256 + (r_ + 1) * 1024], stB[:, 0:1024], cinfo[:, 0:1], stA[:, 0:1024],
                    ALU.mult, ALU.add, ["stA", "stB", "cinfo"], [("uTf", i, "x", r_)])
                stt("dve", uTf[:, i, r_ * 128:(r_ + 1) * 128], stB[:, 1024:1152], cinfo[:, 0:1], stA[:, 1024:1152],
                    ALU.mult, ALU.add, ["stA", "stB", "cinfo"], [("uTf", i, "c", r_)])
        P.barrier()
        uTf_keys = [("uTf", i, a, r_) for i in range(4) for a in ("x", "c") for r_ in range(2)]

        tA = V(OX, [128, LW], F32)
        ki_ = V(OX + 18432, [128, LW], I32)
        tB = V(OX + 18432, [128, LW], F32)
        SINb = [V(OX + 27648, [128, LW], F32), V(OH, [128, LW], F32)]
        COSb = [V(OX + 36864, [128, LW], F32), V(OH + 9216, [128, LW], F32)]
        zmd = V(OH + 18432, [128, LW], F32)
        gsc = V(OH + 27648, [128, LW], BF16)
        SCb = [V(OX + 9216, [128, 2, 2048], BF16), V(OX + 64512, [128, 2, 2048], BF16)]
        tm1 = [V(OX + 46080 + i * 2048, [128, 512], F32) for i in range(2)]
        tm2 = [V(OX + 50176 + i * 2048, [128, 512], F32) for i in range(2)]
        yc = V(OX + 54272, [128, 512], F32)
        y2 = V(OX + 56320, [128, 512], F32)
        y3 = V(OX + 58368, [128, 512], F32)
        ystg = [V(OX + 60416, [128, 2048], BF16)] * 2
        s2v = send2.ap().rearrange("(a r) c -> a (r c)", a=4).rearrange("a (p t) -> a p t", p=128)

        WCH = [(0, 256)] + [(256 + i * 512, 512) for i in range(4)]
        cnt3 = dict(nz=0)
        units = [(ct, d_, g8) for ct in range(4) for d_ in range(2) for g8 in range(8)]

        def tables(u, i):
            ct, d_, g8 = units[u]
            dg = d_ * 32 + ct * 8 + g8
            SIN, COS = SINb[i], COSb[i]
            ks, kc = ("SIN", i), ("COS", i)
            actv(tA[:, :], Atab[:, :], AF.Identity, ["Atab", "psi"], ["tA"], scale=psi[:, dg:dg + 1])
            stt("dve", tA[:, :], Btab[:, :], phi[:, dg:dg + 1], tA[:, :], ALU.mult, ALU.add, ["Btab", "phi", "tA"], ["tA"])
            cpy("dve", ki_[:, :], tA[:, :], ["tA"], ["ki_"])
            tt("dve", tB[:, :], tA[:, :], ki_[:, :], ALU.subtract, ["tA", "ki_"], ["tB", "ki_"])
            actv(SIN[:, :], tB[:, :], AF.Sin, ["tB", "ki_"], [ks], scale=2 * math.pi)
            actv(COS[:, :], tB[:, :], AF.Sin, ["tB", "ki_"], [kc], scale=math.pi)
            actv(COS[:, :], COS[:, :], AF.Square, [kc], [kc])
            actv(COS[:, :], COS[:, :], AF.Identity, [kc, "q25"], [kc], scale=-2.0, bias=q25[:, 0:1])
            cpy("act", SCb[i][:, 0, :], COS[:, 256:LW], [kc], [("SCb", i)])
            cpy("act", SCb[i][:, 1, :], SIN[:, 256:LW], [ks], [("SCb", i)])

        def setup_ctd(ct, d_):
            dg0 = d_ * 32 + ct * 8
            mm(ps[0][:, 0:128], Gm[:, dg0:dg0 + 8, :].rearrange("p a b -> p (a b)"), identF[:, :], True, True,
               ["Gm0", "Gm1", "identF"], [("ps", 0)])
            mm(ps[1][:, 0:128], Gp[:, dg0:dg0 + 8, :].rearrange("p a b -> p (a b)"), identF[:, :], True, True,
               ["Gp0", "Gp1", "identF"], [("ps", 1)])
            for g8 in range(8):
                tsc("dve", BTp[:, g8, :], ps[0][:, 0:128], mask8[:, g8:g8 + 1], None, ALU.mult, None, [("ps", 0), "mask8"], [("BTp", g8)])
                tsc("dve", BTq[:, g8, :], ps[1][:, 0:128], mask8[:, g8:g8 + 1], None, ALU.mult, None, [("ps", 1), "mask8"], [("BTq", g8)])
            mset("pool", ETp[:, :, :], 0.0, [("ETp", g8) for g8 in range(8)])
            mset("pool", ETq[:, :, :], 0.0, [("ETq", g8) for g8 in range(8)])
            etp_d = bass.AP(ETp.tensor, ETp.offset, [list(ETp.ap[0]), [144, 8], [1, 16]])
            etq_d = bass.AP(ETq.tensor, ETq.offset, [list(ETq.ap[0]), [144, 8], [1, 16]])
            cpy("pool", etp_d, Em[:, dg0:dg0 + 8, :], ["Em0", "Em1"] + [("ETp", g8) for g8 in range(8)], [("ETp", g8) for g8 in range(8)])
            cpy("pool", etq_d, Ep[:, dg0:dg0 + 8, :], ["Ep0", "Ep1"] + [("ETq", g8) for g8 in range(8)], [("ETq", g8) for g8 in range(8)])

        def main_unit(u, i):
            ct, d_, g8 = units[u]
            dg = d_ * 32 + ct * 8 + g8
            SIN, COS = SINb[i], COSb[i]
            ks, kc = ("SIN", i), ("COS", i)
            for (n0, ln) in WCH:
                zb = (cnt3["nz"] % 2) * 2
                cnt3["nz"] += 1
                mm(ps[zb][:, 0:ln], BTp[:, g8, :], uTf[:, ct, n0:n0 + ln], True, True, [("BTp", g8)] + uTf_keys, [("ps", zb)])
                mm(ps[zb + 1][:, 0:ln], BTq[:, g8, :], uTf[:, ct, n0:n0 + ln], True, True, [("BTq", g8)] + uTf_keys, [("ps", zb + 1)])
                if d_ == 0:
                    cs, sn, zo = COS[:, n0:n0 + ln], SIN[:, n0:n0 + ln], zmd[:, n0:n0 + ln]
                else:
                    mlo = (255 - (n0 + ln - 1)) if n0 < 256 else (2559 - (n0 + ln - 1))
                    cs, sn, zo = rev(COS[:, mlo:mlo + ln]), rev(SIN[:, mlo:mlo + ln]), rev(zmd[:, mlo:mlo + ln])
                i2 = cnt3["nz"] % 2
                tt("dve", tm1[i2][:, 0:ln], ps[zb][:, 0:ln], cs, ALU.mult, [("ps", zb), kc], [("tm1", i2)])
                tt("dve", tm2[i2][:, 0:ln], ps[zb + 1][:, 0:ln], sn, ALU.mult, [("ps", zb + 1), ks], [("tm2", i2)])
                tt("dve", zo, tm1[i2][:, 0:ln], tm2[i2][:, 0:ln], ALU.add, [("tm1", i2), ("tm2", i2)], ["zmd"])
            P.op("dve", lambda e: e.tensor_tensor_scan(out=gsc[:, :], data0=rdec[:, dg:dg + 1].to_broadcast([128, LW]),
                                                       data1=zmd[:, :], initial=0.0, op0=ALU.mult, op1=ALU.add),
                 ["zmd", "rdec"], ["gsc"])
            tt("dve", M1[:, :], gsc[:, 256:LW], SCb[i][:, 0, :], ALU.mult, ["gsc", ("SCb", i)], ["M1"])
            tt("dve", M2[:, :], gsc[:, 256:LW], SCb[i][:, 1, :], ALU.mult, ["gsc", ("SCb", i)], ["M2"])
            first = (d_ == 0 and g8 == 0)
            for xc in range(4):
                if d_ == 0:
                    r1, r2 = M1[:, xc * 512:(xc + 1) * 512], M2[:, xc * 512:(xc + 1) * 512]
                else:
                    lo = 2048 - (xc + 1) * 512
                    r1, r2 = rev(M1[:, lo:lo + 512]), rev(M2[:, lo:lo + 512])
                if first:
                    mm(ps[4 + xc][:, :], Ddiag[:, ct, :], uTf[:, ct, 256 + xc * 512:256 + (xc + 1) * 512], True, False,
                       [("Ddiag", ct)] + uTf_keys, [("ps", 4 + xc)])
                mm(ps[4 + xc][:, :], ETp[:, g8, :], r1, False, False, [("ETp", g8), "M1"], [("ps", 4 + xc)])
                mm(ps[4 + xc][:, :], ETq[:, g8, :], r2, False, (d_ == 1 and g8 == 7), [("ETq", g8), "M2"], [("ps", 4 + xc)])

        def finish_ct(ct):
            yst = ystg[0]
            for xc in range(4):
                cpy("act", yc[:, :], ps[4 + xc][:, :], [("ps", 4 + xc)], ["yc"])
                tt("dve", y2[:, :], yc[:, :], yc[:, :], ALU.mult, ["yc"], ["y2"])
                tsc("dve", y2[:, :], y2[:, :], 0.044715, 1.0, ALU.mult, ALU.add, ["y2"], ["y2"])
                tt("dve", y2[:, :], y2[:, :], yc[:, :], ALU.mult, ["y2", "yc"], ["y2"])
                actv(y3[:, :], y2[:, :], AF.Sigmoid, ["y2"], ["y3"], scale=2.0 * math.sqrt(2.0 / math.pi))
                tt("dve", yst[:, xc * 512:(xc + 1) * 512], y3[:, :], yc[:, :], ALU.mult, ["y3", "yc"], [("ystg", 0)])
            dma("sp", s2v[ct], yst[:, :], [("ystg", 0)], [("send2", ct)], ("send2", 0))

        tables(0, 0)
        for u in range(len(units)):
            ct, d_, g8 = units[u]
            if u + 1 < len(units):
                tables(u + 1, (u + 1) % 2)
            if g8 == 0:
                setup_ctd(ct, d_)
            main_unit(u, u % 2)
            if d_ == 1 and g8 == 7:
                finish_ct(ct)
        P.coll(lambda e: e.collective_compute("AllGather", ALU.bypass, replica_groups=PAIRS,
                                              ins=[send2.ap().opt()], outs=[recv2.ap().opt()]),
               [("send2", ct) for ct in range(4)], ["recv2"], "cc2")
        P.barrier()

        ygT = V(OS, [128, 8, NX], BF16)
        ga = V(OX, [128, NX], BF16)
        gb = V(OX + 2048, [128, NX], BF16)
        r2v = recv2.ap().rearrange("(a r) c -> a (r c)", a=8).rearrange("a (p t) -> a p t", p=128)
        for a in range(8):
            dma("sp", ga[:, :], r2v[a][:, 0:NX], ["recv2"], ["ga"], "ldga")
            dma("sp", gb[:, :], r2v[a][:, NX:2 * NX], ["recv2"], ["gb"], "ldgb")
            tsc("dve", ga[:, :], ga[:, :], cinfo[:, 1:2], None, ALU.mult, None, ["ga", "cinfo"], ["ga"])
            stt("dve", ygT[:, a, :], gb[:, :], cinfo[:, 0:1], ga[:, :], ALU.mult, ALU.add, ["ga", "gb", "cinfo"], [("ygT", a)])
        P.barrier()
        if dbg and stage == 3:
            d_yg = dram_out("d_yg", [128, 8 * NX], BF16)
            dma("sp", d_yg[:, :], ygT[:, :, :].rearrange("p a b -> p (a b)"), [("ygT", a) for a in range(8)], [], "dbg5")
            P.final_waits += ["dbg5"]

    if stage >= 4:
        wglu_d = dram_in("w_glu", [1024, 1024])
        bglu_d = dram_in("b_gluT", [128, 8])
        wbra_d = dram_in("w_br_attn", [D, D])
        wbrs_d = dram_in("w_br_ssm", [1024, D])
        wout_d = dram_in("w_out", [D, D])
        bglu = sb("bglu_sb", [128, 8], F32)
        dma("sp", bglu[:, :], bglu_d[:, :], [], ["bglu"], "small_bglu")
        mrgT = V(OS + 16384, [128, 16, NX], BF16)
        XCH = [0, 1]
        wgl = V(OX, [128, 8, 1024], BF16)
        gt = [V(OX + 16384 + i * 2048, [128, 512], F32) for i in range(2)]
        y2T = V(OX + 20480, [128, 8, NX], BF16)
        dma("pool", wgl[:, :, :], wglu_d.rearrange("(k p) f -> p k f", p=128), [], ["wgl"], "wgl")
        n4 = 0
        for a in range(8):
            for ci in XCH:
                c0, cl = CH[ci]
                b = n4 % 2
                n4 += 1
                for k in range(8):
                    mm(ps[b][:, :], wgl[:, k, a * 128:(a + 1) * 128], ygT[:, k, c0:c0 + cl], k == 0, k == 7, ["wgl"] + [("ygT", k)], [("ps", b)])
                actv(gt[b][:, :], ps[b][:, :], AF.Sigmoid, [("ps", b), "bglu"], [("gt", b)], bias=bglu[:, a:a + 1], scale=1.0)
                tt("dve", y2T[:, a, c0:c0 + cl], gt[b][:, :], ygT[:, a, c0:c0 + cl], ALU.mult, [("gt", b), ("ygT", a)], [("y2T", a)])
        P.barrier()
        dma("sp", hT[:, :, 0:NX], h2s_v, ["h2s"], [("hT", k, c) for k in range(DK) for c in range(3)], "h2load")
        o = OX + 36864
        wA = V(o, [128, 16, 256], BF16)
        wS = V(o + 8192, [128, 8, 256], BF16)
        wGa = V(o + 12288, [128, 16, 256], BF16)
        wGs = V(o + 20480, [128, 16, 256], BF16)
        m1 = [V(o + 28672 + i * 2048, [128, 512], F32) for i in range(2)]
        m2 = [V(o + 32768 + i * 2048, [128, 512], F32) for i in range(2)]
        wbra_v = wbra_d.rearrange("(k p) f -> p k f", p=128)
        wbrs_v = wbrs_d.rearrange("(k p) f -> p k f", p=128)
        for db2 in range(8):
            c_ = db2 * 256
            dma("pool", wA[:, :, :], wbra_v[:, :, c_:c_ + 256], [], ["wA"], "wA")
            dma("pool", wS[:, :, :], wbrs_v[:, :, c_:c_ + 256], [], ["wS"], "wS")
            dma("pool", wGa[:, :, :], win_v[:, :, 4096 + c_:4096 + c_ + 256], [], ["wGa"], "wGa")
            dma("pool", wGs[:, :, :], win_v[:, :, 6144 + c_:6144 + c_ + 256], [], ["wGs"], "wGs")
            for dl in range(2):
                dk = db2 * 2 + dl
                for ci in XCH:
                    c0, cl = CH[ci]
                    i = n4 % 2
                    n4 += 1
                    bA, bS, bGa, bGs = (0, 1, 2, 3) if i == 0 else (4, 5, 6, 7)
                    for k in range(16):
                        mm(ps[bA][:, :], wA[:, k, dl * 128:(dl + 1) * 128], qT[:, k, c0:c0 + cl], k == 0, k == 15, ["wA", ("qT", k, ci)], [("ps", bA)])
                    for k in range(8):
                        mm(ps[bS][:, :], wS[:, k, dl * 128:(dl + 1) * 128], y2T[:, k, c0:c0 + cl], k == 0, k == 7, ["wS", ("y2T", k)], [("ps", bS)])
                    for k in range(16):
                        mm(ps[bGa][:, :], wGa[:, k, dl * 128:(dl + 1) * 128], hT[:, k, c0:c0 + cl], k == 0, k == 15, ["wGa", ("hT", k, ci)], [("ps", bGa)])
                    for k in range(16):
                        mm(ps[bGs][:, :], wGs[:, k, dl * 128:(dl + 1) * 128], hT[:, k, c0:c0 + cl], k == 0, k == 15, ["wGs", ("hT", k, ci)], [("ps", bGs)])
                    actv(m1[i][:, :], ps[bGa][:, :], AF.Sigmoid, [("ps", bGa)], [("m1", i)])
                    actv(m2[i][:, :], ps[bGs][:, :], AF.Sigmoid, [("ps", bGs)], [("m2", i)])
                    tt("dve", m1[i][:, :], m1[i][:, :], ps[bA][:, :], ALU.mult, [("m1", i), ("ps", bA)], [("m1", i)])
                    tt("dve", m2[i][:, :], m2[i][:, :], ps[bS][:, :], ALU.mult, [("m2", i), ("ps", bS)], [("m2", i)])
                    tt("pool", mrgT[:, dk, c0:c0 + cl], m1[i][:, :], m2[i][:, :], ALU.add, [("m1", i), ("m2", i)], [("mrgT", dk, ci)])
        P.barrier()
        xT2 = V(OX, [128, DK, NX], F32)
        wO = [V(OH + i * 8192, [128, 16, 256], BF16) for i in range(2)]
        xs = [V(OH + 16384 + i * 2048, [128, 512], F32) for i in range(2)]
        wout_v = wout_d.rearrange("(k p) f -> p k f", p=128)
        for db2 in range(8):
            sl = db2 % 2
            dma("pool", wO[sl][:, :, :], wout_v[:, :, db2 * 256:(db2 + 1) * 256], [], [("wO", sl)], ("wO", sl))
            for dl in range(2):
                dk = db2 * 2 + dl
                for ci in XCH:
                    c0, cl = CH[ci]
                    i = n4 % 2
                    n4 += 1
                    dma("sp", xs[i][:, :], x1s_v[:, dk, c0:c0 + cl], [("x1s", dk // 4)], [("xs", i)], ("xs", i))
                    for k in range(16):
                        mm(ps[i][:, :], wO[sl][:, k, dl * 128:(dl + 1) * 128], mrgT[:, k, c0:c0 + cl], k == 0, k == 15,
                           [("wO", sl), ("mrgT", k, ci)], [("ps", i)])
                    stt("dve", xT2[:, dk, c0:c0 + cl], ps[i][:, :], prm[:, G2, dk:dk + 1], xs[i][:, :], ALU.mult, ALU.add,
                        [("ps", i), ("xs", i), ("prm", G2)], [("xT", dk, ci)])
        P.barrier()
        final_x = xT2
        if dbg and stage == 4:
            pass

    if stage >= 5:
        w2g_d = dram_in("w_ffn2_gate", [D, DFF])
        w2u_d = dram_in("w_ffn2_up", [D, DFF])
        w2d_d = dram_in("w_ffn2_down", [DFF, D])
        hT3 = V(OH, [128, DK, NX], BF16)
        norm_mod("n3", xT2, hT3, A3, B3, A3, B3, [0, 1])
        P.barrier()
        ffn("f2", xT2, hT3, NX, w2g_d, w2u_d, w2d_d, G3, G3, [0, 1])
        P.barrier()

    out_v = out_d.rearrange("(k p) t -> p k t", p=128)
    for k4 in range(4 if (stage == 1 or stage >= 4) else 0):
        dma("sp", out_v[:, 4 * k4:4 * k4 + 4, :], final_x[:, 4 * k4:4 * k4 + 4, 0:NX],
            [("xT", k, c) for k in range(4 * k4, 4 * k4 + 4) for c in range(3)], [], "ostore")
    if stage == 1 or stage >= 4:
        P.final_waits.append("ostore")

    P.emit()
    es.close()
    return nc


def prep_inputs(inputs, cores, stage=99):
    x = inputs["x"]; ctx = inputs["ctx"]; c = inputs["c"]; c_ctx = inputs["c_ctx"]
    maps = []
    ca = np.ascontiguousarray
    ng = ca(inputs["norm_g"][0].reshape(3, 16, 128).transpose(2, 0, 1).reshape(128, 48))
    shared = {
        "w_mod": ca(inputs["w_mod"][0]),
        "b_modT": ca(inputs["b_mod"][0].reshape(144, 128).T),
        "ng": ng,
        "w_ffn1_gate": ca(inputs["w_ffn1_gate"][0]),
        "w_ffn1_up": ca(inputs["w_ffn1_up"][0]),
        "w_ffn1_down": ca(inputs["w_ffn1_down"][0]),
    }
    if stage >= 2:
        shared["w_in"] = ca(inputs["w_in"][0])
        shared["qkg"] = ca(np.stack([inputs["q_norm_g"][0], inputs["k_norm_g"][0]], axis=1))
    if stage >= 4:
        shared["w_glu"] = ca(inputs["w_glu"][0])
        shared["b_gluT"] = ca(inputs["b_glu"][0].reshape(8, 128).T)
        shared["w_br_attn"] = ca(inputs["w_br_attn"][0])
        shared["w_br_ssm"] = ca(inputs["w_br_ssm"][0])
        shared["w_out"] = ca(inputs["w_out"][0])
    if stage >= 5:
        shared["w_ffn2_gate"] = ca(inputs["w_ffn2_gate"][0])
        shared["w_ffn2_up"] = ca(inputs["w_ffn2_up"][0])
        shared["w_ffn2_down"] = ca(inputs["w_ffn2_down"][0])
    ssm = {}
    if stage >= 3:
        for j in range(2):
            gs = slice(32 * j, 32 * j + 32)

            def pdg(a):
                t = a.transpose(2, 0, 1).reshape(64, 64)
                return np.concatenate([t, t], 0)
            are = pdg(inputs["ssm_a_re"][0][:, gs, :])
            aim = pdg(inputs["ssm_a_im"][0][:, gs, :])
            ldt = np.broadcast_to(inputs["ssm_log_dt"][0][:, gs].reshape(1, 64), (128, 64))
            sA = np.stack([are, aim, ldt], 1).reshape(128, 192)

            def pB(a):
                t = a.transpose(2, 0, 1, 3).reshape(64, 64, 16)
                return np.concatenate([t, t], 0)
            sB = np.stack([pB(inputs["ssm_b_re"][0][:, gs]), pB(inputs["ssm_b_im"][0][:, gs])], 1).reshape(128, 2048)

            def pC(a):
                t = a.transpose(3, 0, 1, 2).reshape(64, 64, 16)
                return np.concatenate([t, t], 0)
            sC = np.stack([pC(inputs["ssm_c_re"][0][:, gs]), pC(inputs["ssm_c_im"][0][:, gs])], 1).reshape(128, 2048)
            sD = inputs["ssm_d"][0][512 * j:512 * (j + 1)].reshape(4, 128).T
            ssm[j] = dict(ssmA=ca(sA.astype(np.float32)), ssmB=ca(sB), ssmC=ca(sC), ssmD=ca(sD))
    for core in cores:
        b, j = core // 2, core % 2
        xt = np.concatenate([x[b, j * 1024:(j + 1) * 1024], ctx[b, j * 128:(j + 1) * 128]], axis=0)
        cv = np.stack([c[b].reshape(16, 128).T, c_ctx.reshape(16, 128).T], axis=-1).reshape(128, 32)
        m = dict(shared)
        m["xT"] = ca(xt.T)
        m["cvec"] = ca(cv)
        m["cinfo"] = ca(np.tile(np.array([[j, 1 - j, 16 * j, 0]], np.float32), (128, 1)))
        if stage >= 3:
            m.update(ssm[j])
        maps.append(m)
    return maps


def kernel(**inputs):
    inputs = {k: np.asarray(v) for k, v in inputs.items()}
    cores = list(range(N_CORES))
    nc = build_nc()
    maps = prep_inputs(inputs, cores)
    res = run_bass_kernel_spmd(nc, maps, core_ids=cores)
    out = np.zeros((4, 2048, 2048), np.float32)
    for core in cores:
        b, j = core // 2, core % 2
        o = res.results[core]["outT"]
        out[b, j * 1024:(j + 1) * 1024] = o.T
    return out
```

```python
import math
import numpy as np
import concourse.bass as bass
import concourse.mybir as mybir
from concourse.bass_utils import run_bass_kernel_spmd

F32 = mybir.dt.float32
BF16 = mybir.dt.bfloat16
I32 = mybir.dt.int32
AF = mybir.ActivationFunctionType
ALU = mybir.AluOpType

D = 2048
DK = 16
DFF = 5632
NFT = 44
NTOK = 1152
NX = 1024
CH = [(0, 512), (512, 512), (1024, 128)]
EPS = 1e-6
N_CORES = 8
PAIRS = [[0, 1], [2, 3], [4, 5], [6, 7]]


def rev(ap, n=None):
    pat = [list(x) for x in ap.ap]
    fs, fc = pat[-1]
    pat[-1] = [-fs, fc]
    return bass.AP(ap.tensor, ap.offset + (fc - 1) * fs, pat)


class Prog:
    ENGS = ["pe", "act", "dve", "pool", "sp"]

    def __init__(self, nc):
        self.nc = nc
        self.ops = {e: [] for e in self.ENGS}
        self.lastw = {}
        self.readers = {}
        self.dma_keys = {}
        self.final_waits = []

    def _add(self, eng, fn, r, w, dma_key):
        r = list(r) + ["__BAR__"]
        deps = set()
        for k in r:
            x = self.lastw.get(k)
            if x is not None:
                deps.add(x)
        for k in w:
            x = self.lastw.get(k)
            if x is not None:
                deps.add(x)
            for y in self.readers.get(k, ()):
                deps.add(y)
        idx = len(self.ops[eng])
        me = (eng, idx)
        deps.discard(me)
        if dma_key is not None:
            cnt = self.dma_keys.get(dma_key, 0) + 1
            self.dma_keys[dma_key] = cnt
        else:
            cnt = None
        self.ops[eng].append(dict(fn=fn, deps=deps, dma_key=dma_key, dma_cnt=cnt, sig=False))
        for k in r:
            self.readers.setdefault(k, []).append(me)
        for k in w:
            self.lastw[k] = me
            self.readers[k] = []
        return me

    def op(self, eng, fn, r=(), w=()):
        return self._add(eng, fn, r, w, None)

    def barrier(self):
        self._add("sp", lambda e: e.nop(), [], ["__BAR__"], None)

    def coll(self, fn, r, w, key):
        me = self._add("pool", fn, r, w, key)
        self.ops["pool"][me[1]]["inc"] = 1
        return me

    def dma(self, eng, fn, r=(), w=(), key=None):
        assert key is not None
        return self._add(eng, fn, r, w, key)

    def emit(self):
        nc = self.nc
        ops = self.ops
        for e in self.ENGS:
            for o in ops[e]:
                for (e2, i2) in o["deps"]:
                    o2 = ops[e2][i2]
                    if o2["dma_key"] is None:
                        if e2 == "pe" and e == "pe":
                            continue
                        o2["sig"] = True
        for e in self.ENGS:
            c = 0
            for o in ops[e]:
                if o["dma_key"] is None and o["sig"]:
                    c += 1
                    o["sigval"] = c
        import contextlib
        with contextlib.ExitStack() as st:
            csem = {e: st.enter_context(nc.semaphore("c_" + e)) for e in self.ENGS}
            dsem = {k: st.enter_context(nc.semaphore("d_%d" % i)) for i, k in enumerate(self.dma_keys)}
            block = st.enter_context(nc.Block())

            def gen(e):
                def body(eng):
                    have = {}
                    for o in ops[e]:
                        need = {}
                        for (e2, i2) in o["deps"]:
                            o2 = ops[e2][i2]
                            if o2["dma_key"] is not None:
                                kk = ("d", o2["dma_key"])
                                v = o2.get("inc", 16) * o2["dma_cnt"]
                            else:
                                if e2 == "pe" and e == "pe":
                                    continue
                                kk = ("c", e2)
                                v = o2["sigval"]
                            if need.get(kk, 0) < v:
                                need[kk] = v
                        for kk, v in need.items():
                            if have.get(kk, 0) >= v:
                                continue
                            have[kk] = v
                            sem = dsem[kk[1]] if kk[0] == "d" else csem[kk[1]]
                            eng.wait_ge(sem, v)
                        ins = o["fn"](eng)
                        if o["dma_key"] is not None:
                            if o.get("inc", 16) == 1:
                                ins.then_inc(dsem[o["dma_key"]])
                            else:
                                ins.then_inc(dsem[o["dma_key"]], 16)
                        elif o["sig"]:
                            ins.then_inc(csem[e], 1)
                    if e == "sp":
                        for k in self.final_waits:
                            eng.wait_ge(dsem[k], 16 * self.dma_keys[k])
                return body

            block.tensor(gen("pe"))
            block.scalar(gen("act"))
            block.vector(gen("dve"))
            block.gpsimd(gen("pool"))
            block.sync(gen("sp"))


def bc_last(ap, n):
    pat = [list(x) for x in ap.ap] + [[0, n]]
    return bass.AP(ap.tensor, ap.offset, pat)


def build_nc(stage=99, NG=11, dbg=False, pairs=None):
    global PAIRS
    if pairs is not None:
        PAIRS = pairs
    import contextlib
    nc = bass.Bass("TRN2", target_bir_lowering=False)
    P = Prog(nc)
    es = contextlib.ExitStack()

    def dram_in(name, shape, dt=F32):
        return nc.dram_tensor(name, list(shape), dt, kind="ExternalInput").ap()

    def dram_out(name, shape, dt=F32):
        return nc.dram_tensor(name, list(shape), dt, kind="ExternalOutput").ap()

    xT_d = dram_in("xT", [D, NTOK])
    cvec_d = dram_in("cvec", [128, 32])
    wmod_d = dram_in("w_mod", [D, 9 * D])
    bmod_d = dram_in("b_modT", [128, 144])
    ng_d = dram_in("ng", [128, 48])
    w1g_d = dram_in("w_ffn1_gate", [D, DFF])
    w1u_d = dram_in("w_ffn1_up", [D, DFF])
    w1d_d = dram_in("w_ffn1_down", [DFF, D])
    out_d = dram_out("outT", [D, NX])

    ARENA = 204800
    arena = es.enter_context(nc.sbuf_tensor("arena", [128, ARENA // 2], BF16))

    def V(off, shape, dt):
        esz = 4 if dt in (F32, I32) else 2
        n = 1
        for d_ in shape[1:]:
            n *= d_
        assert off % 4 == 0 and off + n * esz <= ARENA, (off, shape)
        ap = arena[0:shape[0], off // 2: off // 2 + n * esz // 2]
        if dt != BF16:
            ap = ap.bitcast(dt)
        if len(shape) == 3:
            ap = ap.rearrange("p (a b) -> p a b", b=shape[2])
        elif len(shape) == 4:
            ap = ap.rearrange("p (a b c) -> p a b c", b=shape[2], c=shape[3])
        return ap

    OX = 0
    OH = 73728
    OQ = 110592
    OS = 143360

    xT = V(OX, [128, DK, NTOK], F32)
    hT = V(OH, [128, DK, NTOK], BF16)

    def sb(name, shape, dt):
        return es.enter_context(nc.sbuf_tensor(name, list(shape), dt))

    modT = sb("modT", [128, 144, 2], F32)
    ng = sb("ng_sb", [128, 3, DK], F32)
    prm = sb("prm", [128, 16, DK], F32)
    onesb = sb("onesb", [128, 128], BF16)
    ident2 = sb("ident2", [2, 2], F32)
    epsb = sb("epsb", [128, 1], F32)
    cinfo = sb("cinfo_sb", [128, 4], F32)
    ps = [es.enter_context(nc.psum_tensor("ps%d" % i, [128, 512], F32)) for i in range(8)]

    A1, B1, G1, A1c, B1c, G1c, A2, B2, G2, A2c, B2c, A3, B3, G3 = range(14)

    def mm(out, lhsT, rhs, start, stop, r, w):
        P.op("pe", lambda e: e.matmul(out, lhsT=lhsT, rhs=rhs, start=start, stop=stop), r, w)

    def actv(out, in_, func, r, w, bias=None, scale=None):
        kw = {}
        if bias is not None:
            kw["bias"] = bias
        if scale is not None:
            kw["scale"] = scale
        P.op("act", lambda e: e.activation(out=out, in_=in_, func=func, **kw), r, w)

    def tt(eng, out, in0, in1, op, r, w):
        P.op(eng, lambda e: e.tensor_tensor(out=out, in0=in0, in1=in1, op=op), r, w)

    def stt(eng, out, in0, scalar, in1, op0, op1, r, w):
        P.op(eng, lambda e: e.scalar_tensor_tensor(out=out, in0=in0, scalar=scalar, in1=in1, op0=op0, op1=op1), r, w)

    def tsc(eng, out, in0, s1, s2, op0, op1, r, w):
        if s2 is None:
            P.op(eng, lambda e: e.tensor_scalar(out=out, in0=in0, scalar1=s1, scalar2=None, op0=op0), r, w)
        else:
            P.op(eng, lambda e: e.tensor_scalar(out=out, in0=in0, scalar1=s1, scalar2=s2, op0=op0, op1=op1), r, w)

    def cpy(eng, out, in_, r, w):
        if eng == "act":
            P.op(eng, lambda e: e.activation(out=out, in_=in_, func=AF.Identity), r, w)
        else:
            P.op(eng, lambda e: e.tensor_copy(out=out, in_=in_), r, w)

    def recip(out, in_, r, w):
        P.op("dve", lambda e: e.reciprocal(out=out, in_=in_), r, w)

    def mset(eng, ap, val, w):
        P.op(eng, lambda e: e.memset(ap, val), (), w)

    def asel(out, pattern, cmp, base, cm, r, w, fill=0.0):
        P.op("pool", lambda e: e.affine_select(out=out, in_=out, pattern=pattern, compare_op=cmp, fill=fill,
                                               base=base, channel_multiplier=cm), r, w)

    def dma(eng, out, in_, r, w, key):
        P.dma(eng, lambda e: e.dma_start(out=out, in_=in_), r, w, key)

    mset("pool", onesb[:, :], 1.0 / D, ["onesb"])
    mset("pool", epsb[:, :], EPS, ["epsb"])
    mset("pool", prm[:, :, :], 0.0, [("prm", i) for i in range(16)])
    mset("pool", ident2[:, :], 1.0, ["ident2"])
    asel(ident2[:, :], [[-1, 2]], ALU.is_equal, 0, 1, ["ident2"], ["ident2"])
    cinfo_d = dram_in("cinfo", [128, 4])
    dma("sp", cinfo[:, :], cinfo_d[:, :], [], ["cinfo"], "small_cinfo")

    xT_v = xT_d.rearrange("(k p) t -> p k t", p=128)
    for k4 in range(4):
        dma("sp", xT[:, 4 * k4:4 * k4 + 4, :], xT_v[:, 4 * k4:4 * k4 + 4, :], [],
            [("xT", k, c) for k in range(4 * k4, 4 * k4 + 4) for c in range(3)], ("xload", k4))
    dma("sp", ng[:, :, :].rearrange("p a b -> p (a b)"), ng_d[:, :], [], ["ng"], "small_ng")

    cv = V(OQ, [128, 32], F32)
    csil = sb("csil_sb", [128, 32], BF16)
    bmT = sb("bmT_sb", [128, 144], F32)
    wmb = [V(OQ + 1024 + i * 16384, [128, DK, 512], BF16) for i in range(2)]
    mrow = [V(OQ + 1024 + 32768 + i * 2048, [2, 512], F32) for i in range(2)]
    dma("sp", cv[:, :], cvec_d[:, :], [], ["cv"], "small_cv")
    dma("sp", bmT[:, :], bmod_d[:, :], [], ["bmT"], "small_bm")
    actv(csil[:, :], cv[:, :], AF.Silu, ["cv"], ["csil"])
    wmod_v = wmod_d.rearrange("(k p) f -> p k f", p=128)
    modT_ps = ps[7]
    for cb in range(12):
        wb = wmb[cb % 2]
        dma("pool", wb[:, :, :], wmod_v[:, :, cb * 512:(cb + 1) * 512], [], [("wmb", cb % 2)], ("wmb", cb % 2))
        pst = ps[cb % 2]
        for k in range(DK):
            mm(pst[0:2, :], csil[:, 2 * k:2 * k + 2], wb[:, k, :], k == 0, k == DK - 1,
               ["csil", ("wmb", cb % 2)], [("ps", cb % 2)])
        mr = mrow[cb % 2]
        cpy("dve", mr[:, :], pst[0:2, :], [("ps", cb % 2)], [("mrow", cb % 2)])
        for q in range(4):
            t = cb * 4 + q
            mm(modT_ps[:, 2 * t:2 * t + 2], mr[:, q * 128:(q + 1) * 128], ident2[:, :], True, True,
               [("mrow", cb % 2), "ident2"], ["modT_ps"])
    mps_v = modT_ps[:, 0:288].rearrange("p (a b) -> p a b", b=2)
    for col in range(2):
        tt("dve", modT[:, 0:48, col], mps_v[:, 0:48, col], bmT[:, 0:48], ALU.add, ["modT_ps", "bmT"], ["modT"])
    P.barrier()

    wmL = [V(182272 + i * 8192, [128, DK, 256], BF16) for i in range(2)]
    mrL = [V(198656 + i * 1024, [2, 256], F32) for i in range(2)]
    late = dict(nxt=0, pend=None, tr=None)

    def late_mod_step(issue=True):
        bt = late["tr"]
        if bt is not None:
            for q in range(2):
                t = 48 + 2 * bt + q
                mm(modT_ps[:, 2 * t:2 * t + 2], mrL[bt % 2][:, q * 128:(q + 1) * 128], ident2[:, :], True, True,
                   [("mrL", bt % 2), "ident2"], ["modT_ps"])
            late["tr"] = None
        b = late["pend"]
        if b is not None:
            wb_ = wmL[b % 2]
            for k in range(DK):
                mm(ps[6][0:2, 0:256], csil[:, 2 * k:2 * k + 2], wb_[:, k, :], k == 0, k == DK - 1,
                   ["csil", ("wmL", b % 2)], [("ps", 6)])
            cpy("dve", mrL[b % 2][:, :], ps[6][0:2, 0:256], [("ps", 6)], [("mrL", b % 2)])
            late["tr"] = b
            late["pend"] = None
        if issue and late["nxt"] < 48:
            b = late["nxt"]
            late["nxt"] += 1
            c_ = 6144 + 256 * b
            dma("pool", wmL[b % 2][:, :, :], wmod_v[:, :, c_:c_ + 256], [], [("wmL", b % 2)], ("wmL", b % 2))
            late["pend"] = b

    def mv(n, col):
        return modT[:, n * 16:(n + 1) * 16, col]

    def derive_A(slot, gi, n, col):
        stt("dve", prm[:, slot, :], mv(n, col), 1.0, ng[:, gi, :], ALU.add, ALU.mult, ["modT", "ng"], [("prm", slot)])

    def derive_copy(slot, n, col, mul=1.0):
        tsc("dve", prm[:, slot, :], mv(n, col), mul, None, ALU.mult, None, ["modT"], [("prm", slot)])

    derive_A(A1, 0, 1, 0); derive_copy(B1, 0, 0); derive_copy(G1, 2, 0, 0.5)
    derive_A(A1c, 0, 1, 1); derive_copy(B1c, 0, 1); derive_copy(G1c, 2, 1, 0.5)

    def norm_mod(tag, xv, hv, Ax, Bx, Ac, Bc, chunks):
        sq = [V(OQ + i * 1024, [128, 512], BF16) for i in range(3)]
        rstd = V(OQ + 3072, [128, 512], F32)
        tmp = [V(OQ + 5120 + i * 2048, [128, 512], F32) for i in range(2)]
        for ci in chunks:
            c0, cl = CH[ci]
            pst = ps[6]
            for k in range(DK):
                s_ = sq[k % 3]
                actv(s_[:, 0:cl], xv[:, k, c0:c0 + cl], AF.Square, [("xT", k, ci)], [("sq", tag, k % 3)])
                mm(pst[:, 0:cl], onesb[:, :], s_[:, 0:cl], k == 0, k == DK - 1, [("sq", tag, k % 3), "onesb"], [("ps", 6)])
            actv(rstd[:, 0:cl], pst[:, 0:cl], AF.Sqrt, [("ps", 6), "epsb"], [("rstd", tag)], bias=epsb[:, 0:1], scale=1.0)
            recip(rstd[:, 0:cl], rstd[:, 0:cl], [("rstd", tag)], [("rstd", tag)])
            a_s, b_s = (Ac, Bc) if ci == 2 else (Ax, Bx)
            for k in range(DK):
                t = tmp[k % 2]
                tt("dve", t[:, 0:cl], xv[:, k, c0:c0 + cl], rstd[:, 0:cl], ALU.mult,
                   [("xT", k, ci), ("rstd", tag)], [("ntmp", tag, k % 2)])
                actv(hv[:, k, c0:c0 + cl], t[:, 0:cl], AF.Identity,
                     [("ntmp", tag, k % 2), ("prm", a_s), ("prm", b_s)], [("hT", k, ci)],
                     bias=prm[:, b_s, k:k + 1], scale=prm[:, a_s, k:k + 1])

    def ffn(tag, xv, hv, ntok, wg_d, wu_d, wd_d, Gx, Gc, chunks, hook=None):
        wg_v = wg_d.rearrange("(k p) f -> p k f", p=128)
        wu_v = wu_d.rearrange("(k p) f -> p k f", p=128)
        wd_v = wd_d.rearrange("(k p) d -> p k d", p=128)
        o = OQ
        wgb = [V(o + i * 8192, [128, DK, 256], BF16) for i in range(2)]
        wub = [V(o + 16384 + i * 8192, [128, DK, 256], BF16) for i in range(2)]
        wdb = [V(o + 32768 + i * 8192, [128, 4, 1024], BF16) for i in range(2)]
        act = [V(o + 49152 + i * 9216, [128, 4, ntok], BF16) for i in range(2)]
        stmp = [V(o + 67584 + i * 2048, [128, 512], F32) for i in range(2)]
        cnt = dict(nb=0, nd=0, ne=0)

        def gu_group(g):
            a = act[g % 2]
            for half in range(2):
                slot = cnt["nb"] % 2
                cnt["nb"] += 1
                f0 = g * 512 + half * 256
                dma("pool", wgb[slot][:, :, :], wg_v[:, :, f0:f0 + 256], [], [("wgb", tag, slot)], ("wgb", tag, slot))
                dma("pool", wub[slot][:, :, :], wu_v[:, :, f0:f0 + 256], [], [("wub", tag, slot)], ("wub", tag, slot))
                for fl in range(2):
                    fi = half * 2 + fl
                    if hook is not None:
                        hook()
                    for ci in chunks:
                        c0, cl = CH[ci]
                        ne = cnt["ne"]
                        cnt["ne"] += 1
                        bg = (ne % 2) * 2
                        pg, pu = ps[bg], ps[bg + 1]
                        for k in range(DK):
                            mm(pg[:, 0:cl], wgb[slot][:, k, fl * 128:(fl + 1) * 128], hv[:, k, c0:c0 + cl], k == 0, k == DK - 1,
                               [("wgb", tag, slot), ("hT", k, ci)], [("ps", bg)])
                        for k in range(DK):
                            mm(pu[:, 0:cl], wub[slot][:, k, fl * 128:(fl + 1) * 128], hv[:, k, c0:c0 + cl], k == 0, k == DK - 1,
                               [("wub", tag, slot), ("hT", k, ci)], [("ps", bg + 1)])
                        st_ = stmp[ne % 2]
                        actv(st_[:, 0:cl], pg[:, 0:cl], AF.Silu, [("ps", bg)], [("stmp", tag, ne % 2)])
                        tt("dve", a[:, fi, c0:c0 + cl], st_[:, 0:cl], pu[:, 0:cl], ALU.mult,
                           [("stmp", tag, ne % 2), ("ps", bg + 1)], [("act", tag, g % 2, fi, ci)])

        def down_group(g):
            a = act[g % 2]
            for dh in range(2):
                slot = cnt["nd"] % 2
                cnt["nd"] += 1
                dma("pool", wdb[slot][:, :, :], wd_v[:, 4 * g:4 * g + 4, dh * 1024:(dh + 1) * 1024], [],
                    [("wdb", tag, slot)], ("wdb", tag, slot))
                for dl in range(8):
                    dk = dh * 8 + dl
                    for ci in chunks:
                        c0, cl = CH[ci]
                        bank = 4 + ((dk * 3 + ci) % 2)
                        pd = ps[bank]
                        for fi in range(4):
                            mm(pd[:, 0:cl], wdb[slot][:, fi, dl * 128:(dl + 1) * 128], a[:, fi, c0:c0 + cl], fi == 0, fi == 3,
                               [("wdb", tag, slot), ("act", tag, g % 2, fi, ci)], [("ps", bank)])
                        gs = Gc if ci == 2 else Gx
                        stt("dve", xv[:, dk, c0:c0 + cl], pd[:, 0:cl], prm[:, gs, dk:dk + 1], xv[:, dk, c0:c0 + cl],
                            ALU.mult, ALU.add, [("ps", bank), ("prm", gs), ("xT", dk, ci)], [("xT", dk, ci)])

        gu_group(0)
        for g in range(1, NG):
            gu_group(g)
            down_group(g - 1)
        down_group(NG - 1)

    norm_mod("n1", xT, hT, A1, B1, A1c, B1c, [0, 1, 2])
    P.barrier()
    ffn("f1", xT, hT, NTOK, w1g_d, w1u_d, w1d_d, G1, G1c, [0, 1, 2], hook=late_mod_step)
    while late["pend"] is not None or late["tr"] is not None or late["nxt"] < 48:
        late_mod_step()
    for col in range(2):
        tt("dve", modT[:, 48:144, col], mps_v[:, 48:144, col], bmT[:, 48:144], ALU.add, ["modT_ps", "bmT"], ["modT"])
    derive_A(A2, 1, 4, 0); derive_copy(B2, 3, 0); derive_copy(G2, 5, 0)
    derive_A(A2c, 1, 4, 1); derive_copy(B2c, 3, 1)
    derive_A(A3, 2, 7, 0); derive_copy(B3, 6, 0); derive_copy(G3, 8, 0, 0.5)
    P.barrier()
    final_x = xT

    if dbg and stage == 1:
        d_mod = dram_out("d_mod", [128, 288])
        d_hT = dram_out("d_hT", [128, DK * NTOK], BF16)
        dma("sp", d_mod[:, :], modT[:, :, :].rearrange("p a b -> p (a b)"), ["modT"], [], "dbg1")
        dma("sp", d_hT[:, :], hT[:, :, :].rearrange("p a b -> p (a b)"), [("hT", k, c) for k in range(DK) for c in range(3)], [], "dbg3")
        P.final_waits += ["dbg1", "dbg3"]

    if stage >= 2:
        HD = 128
        win_d = dram_in("w_in", [D, 8192])
        qkg_d = dram_in("qkg", [128, 2])
        sK = nc.dram_tensor("sK", [576, 1024], BF16); rK = nc.dram_tensor("rK", [1152, 1024], BF16)
        sV = nc.dram_tensor("sV", [576, 1024], BF16); rV = nc.dram_tensor("rV", [1152, 1024], BF16)
        sU = [nc.dram_tensor("sU%d" % i, [576, 1024], BF16) for i in range(2)]
        rU = [nc.dram_tensor("rU%d" % i, [1152, 1024], BF16) for i in range(2)]
        x1s = nc.dram_tensor("x1s", [128, DK * NX], F32)
        h2s = nc.dram_tensor("h2s", [128, DK * NX], BF16)
        win_v = win_d.rearrange("(k p) f -> p k f", p=128)

        qT = V(OQ, [128, 16, NX], BF16)
        qkg = sb("qkg_sb", [128, 3], F32)
        ones1 = sb("ones1", [128, 128], BF16)
        onesh = sb("onesh", [128, 128], BF16)
        Pm = sb("Pm", [128, 128], BF16)
        dma("sp", qkg[:, 0:2], qkg_d[:, :], [], ["qkg"], "small_qkg")
        tsc("dve", qkg[:, 2:3], qkg[:, 0:1], HD ** -0.5, None, ALU.mult, None, ["qkg"], ["qkg2"])
        mset("pool", ones1[:, :], 1.0, ["ones1"])
        mset("pool", onesh[:, :], 1.0 / HD, ["onesh"])

        norm_mod("n2", xT, hT, A2, B2, A2c, B2c, [0, 1, 2])
        x1s_v = x1s.ap().rearrange("p (k t) -> p k t", t=NX)
        for k4 in range(4):
            dma("sp", x1s_v[:, 4 * k4:4 * k4 + 4, :], xT[:, 4 * k4:4 * k4 + 4, 0:NX],
                [("xT", k, c) for k in range(4 * k4, 4 * k4 + 4) for c in range(3)], [("x1s", k4)], ("x1spill", k4))
        P.barrier()

        cosT = V(OX, [128, NX], F32)
        sinT = V(OX + 4096, [128, NX], F32)
        o = OX + 8192
        wblk = [V(o + i * 16384, [128, DK, 512], BF16) for i in range(2)]
        o += 32768
        pid_i = V(o, [128, 1], I32)
        pidf = V(o + 64, [128, 1], F32)
        invf = V(o + 128, [128, 1], F32)
        pos_i = V(o + 256, [128, NX], I32)
        posf = V(o + 256 + 4096, [128, NX], F32)
        turn = V(o + 256 + 8192, [128, NX], F32)
        ki = V(o + 256 + 12288, [128, NX], I32)
        pa = V(o + 256 + 16384, [128, 128], F32)
        pb_ = V(o + 256 + 16384 + 512, [128, 128], F32)
        P.op("pool", lambda e: e.iota(pid_i[:, :], pattern=[[0, 1]], base=0, channel_multiplier=1), [], ["pid_i"])
        P.op("dve", lambda e: e.tensor_single_scalar(out=pid_i[:, :], in_=pid_i[:, :], scalar=31, op=ALU.bitwise_and), ["pid_i"], ["pid_i"])
        cpy("pool", pidf[:, :], pid_i[:, :], ["pid_i"], ["pidf"])
        actv(invf[:, :], pidf[:, :], AF.Exp, ["pidf"], ["invf"], scale=-math.log(10000.0) / 32.0)
        P.op("pool", lambda e: e.iota(pos_i[0:64, :].rearrange("p (a b) -> p a b", b=64), pattern=[[1, 16], [0, 64]], base=0, channel_multiplier=0), [], ["pos_i0"])
        P.op("pool", lambda e: e.iota(pos_i[64:128, :].rearrange("p (a b) -> p a b", b=64), pattern=[[0, 16], [1, 64]], base=0, channel_multiplier=0), [], ["pos_i1"])
        cpy("pool", posf[:, :], pos_i[:, :], ["pos_i0", "pos_i1"], ["posf"])
        tsc("pool", posf[0:64, :], posf[0:64, :], cinfo[0:64, 2:3], None, ALU.add, None, ["posf", "cinfo"], ["posf"])
        tsc("dve", turn[:, :], posf[:, :], invf[:, 0:1], 1.0 / (2 * math.pi), ALU.mult, ALU.mult, ["posf", "invf"], ["turn"])
        cpy("dve", ki[:, :], turn[:, :], ["turn"], ["ki"])
        tt("dve", posf[:, :], turn[:, :], ki[:, :], ALU.subtract, ["turn", "ki"], ["posf"])
        actv(sinT[:, :], posf[:, :], AF.Sin, ["posf"], ["sinT"], scale=2 * math.pi)
        tsc("dve", turn[:, :], turn[:, :], 0.25, None, ALU.add, None, ["turn"], ["turn"])
        cpy("dve", ki[:, :], turn[:, :], ["turn"], ["ki"])
        tt("dve", posf[:, :], turn[:, :], ki[:, :], ALU.subtract, ["turn", "ki", "sinT"], ["posf"])
        actv(cosT[:, :], posf[:, :], AF.Sin, ["posf"], ["cosT"], scale=2 * math.pi)
        mset("pool", pa[:, :], -1.0, ["pa"])
        asel(pa[:, :], [[-1, 128]], ALU.is_equal, -32, 1, ["pa"], ["pa"])
        mset("pool", pa[:, 32:64], 0.0, ["pa"])
        mset("pool", pa[:, 96:128], 0.0, ["pa"])
        mset("pool", pb_[:, :], 1.0, ["pb_"])
        asel(pb_[:, :], [[-1, 128]], ALU.is_equal, 32, 1, ["pb_"], ["pb_"])
        mset("pool", pb_[:, 0:32], 0.0, ["pb_"])
        mset("pool", pb_[:, 64:96], 0.0, ["pb_"])
        tt("pool", Pm[:, :], pa[:, :], pb_[:, :], ALU.add, ["pa", "pb_"], ["Pm"])
        P.barrier()

        kst = [V(o + i * 2304, [128, NTOK], BF16) for i in range(2)]
        vst = [V(o + 4608 + i * 1024, [128, 512], BF16) for i in range(2)]
        ust = [V(o + 6656 + i * 2304, [128, NTOK], BF16) for i in range(2)]
        sqb = [V(o + 11264 + i * 1024, [128, 512], BF16) for i in range(3)]
        rsb = [V(o + 14336 + i * 2048, [128, 512], F32) for i in range(3)]
        knb = [V(o + 20480 + i * 1024, [128, 512], BF16) for i in range(3)]
        t1b = [V(o + 23552 + i * 2048, [128, 512], F32) for i in range(2)]
        t2b = [V(o + 27648 + i * 2048, [128, 512], F32) for i in range(2)]
        sb_k = sK.ap().rearrange("(h r) c -> h (r c)", h=4).rearrange("h (d t) -> h d t", d=128)
        sb_v = sV.ap().rearrange("(t r) c -> t (r c)", t=9).rearrange("t (p c) -> t p c", p=128)
        sb_u = [sU[i].ap().rearrange("(a r) c -> a (r c)", a=4).rearrange("a (p t) -> a p t", p=128) for i in range(2)]

        def allgather(src, dst, rkeys, wkey, key):
            P.coll(lambda e: e.collective_compute("AllGather", ALU.bypass, replica_groups=PAIRS,
                                                  ins=[src.ap().opt()], outs=[dst.ap().opt()]), rkeys, [wkey], key)

        cnt2 = dict(w=0, n=0, b=0)

        def load_w(col0):
            slot = cnt2["w"] % 2
            cnt2["w"] += 1
            dma("pool", wblk[slot][:, :, :], win_v[:, :, col0:col0 + 512], [], [("wblk", slot)], ("wblk", slot))
            return slot

        pipe = []

        def qk_unit(slot, ctl, ci, gain, dest, dkey, rope, post=None):
            c0, cl = CH[ci]
            n = cnt2["n"]
            cnt2["n"] += 1
            i3, i2 = n % 3, n % 2
            pb, pkey = ps[i3], ("ps", i3)

            def stA():
                for k in range(DK):
                    mm(pb[:, 0:cl], wblk[slot][:, k, ctl * 128:(ctl + 1) * 128], hT[:, k, c0:c0 + cl], k == 0, k == DK - 1,
                       [("wblk", slot), ("hT", k, ci)], [pkey])
                actv(sqb[i3][:, 0:cl], pb[:, 0:cl], AF.Square, [pkey], [("sqb", i3)])

            def stB():
                mm(ps[3 + i2][:, 0:cl], onesh[:, :], sqb[i3][:, 0:cl], True, True, [("sqb", i3), "onesh"], [("ps", 3 + i2)])
                actv(rsb[i3][:, 0:cl], ps[3 + i2][:, 0:cl], AF.Sqrt, [("ps", 3 + i2), "epsb"], [("rsb", i3)], bias=epsb[:, 0:1], scale=1.0)
                recip(rsb[i3][:, 0:cl], rsb[i3][:, 0:cl], [("rsb", i3)], [("rsb", i3)])
                if not rope:
                    stt("dve", dest, pb[:, 0:cl], gain, rsb[i3][:, 0:cl], ALU.mult, ALU.mult, [pkey, ("rsb", i3), "qkg", "qkg2"], [dkey])
                else:
                    stt("dve", knb[i3][:, 0:cl], pb[:, 0:cl], gain, rsb[i3][:, 0:cl], ALU.mult, ALU.mult, [pkey, ("rsb", i3), "qkg", "qkg2"], [("knb", i3)])

            def stC():
                if rope:
                    mm(ps[5 + i2][:, 0:cl], Pm[:, :], knb[i3][:, 0:cl], True, True, [("knb", i3), "Pm"], [("ps", 5 + i2)])
                    tt("dve", t1b[i2][:, 0:cl], knb[i3][:, 0:cl], cosT[:, c0:c0 + cl], ALU.mult, [("knb", i3), "cosT"], [("t1b", i2)])
                    tt("dve", t2b[i2][:, 0:cl], ps[5 + i2][:, 0:cl], sinT[:, c0:c0 + cl], ALU.mult, [("ps", 5 + i2), "sinT"], [("t2b", i2)])
                    tt("dve", dest, t1b[i2][:, 0:cl], t2b[i2][:, 0:cl], ALU.add, [("t1b", i2), ("t2b", i2)], [dkey])
                if post is not None:
                    post()
            pipe.append([stA, stB, stC])
            advance()

        def advance(flush=False):
            if not flush:
                pipe[-1][0]()
                if len(pipe) >= 2:
                    pipe[-2][1]()
                if len(pipe) >= 3:
                    pipe[-3][2]()
            else:
                if len(pipe) >= 1:
                    pipe[-1][1]()
                if len(pipe) >= 2:
                    pipe[-2][2]()
                if len(pipe) >= 1:
                    pipe[-1][2]()
                del pipe[:]

        def proj_fm(slot, ctl, ci):
            b = cnt2["b"] % 2
            cnt2["b"] += 1
            c0, cl = CH[ci]
            for k in range(DK):
                mm(ps[b][:, 0:cl], wblk[slot][:, k, ctl * 128:(ctl + 1) * 128], hT[:, k, c0:c0 + cl], k == 0, k == DK - 1,
                   [("wblk", slot), ("hT", k, ci)], [("ps", b)])
            return b

        slot = load_w(0)
        for h in range(4):
            for ci in range(3):
                c0, cl = CH[ci]
                post = None
                if ci == 2:
                    def post(h=h):
                        dma("sp", sb_k[h], kst[h % 2][:, :], [("kst", h % 2, c_) for c_ in range(3)], [("sendb", "k", h)], ("sendk", h % 2))
                qk_unit(slot, h, ci, qkg[:, 1:2], kst[h % 2][:, c0:c0 + cl], ("kst", h % 2, ci), ci < 2, post)
        advance(flush=True)
        allgather(sK, rK, [("sendb", "k", h) for h in range(4)], "rK", "cc1k")
        slot = load_w(512)
        for t9 in range(9):
            ci = 0 if t9 < 4 else (1 if t9 < 8 else 2)
            b = 7
            for k in range(DK):
                mm(ps[b][:, :], hT[:, k, t9 * 128:(t9 + 1) * 128], wblk[slot][:, k, :], k == 0, k == DK - 1,
                   [("wblk", slot), ("hT", k, ci)], [("ps", b)])
            cpy("act", vst[t9 % 2][:, :], ps[b][:, :], [("ps", b)], [("vst", t9 % 2)])
            dma("sp", sb_v[t9], vst[t9 % 2][:, :], [("vst", t9 % 2)], [("sendb", "v", t9)], ("sendv", t9 % 2))
        allgather(sV, rV, [("sendb", "v", t9) for t9 in range(9)], "rV", "cc1v")
        for ub in range(2):
            slot = load_w(1024 + ub * 512)
            for ctl in range(4):
                for ci in range(3):
                    c0, cl = CH[ci]
                    b = proj_fm(slot, ctl, ci)
                    cpy("act", ust[ctl % 2][:, c0:c0 + cl], ps[b][:, 0:cl], [("ps", b)], [("ust", ctl % 2, ci)])
                dma("sp", sb_u[ub][ctl], ust[ctl % 2][:, :], [("ust", ctl % 2, ci) for ci in range(3)], [("sendb", "u", ub * 4 + ctl)], ("sendu", ctl % 2))
            allgather(sU[ub], rU[ub], [("sendb", "u", ub * 4 + ctl) for ctl in range(4)], ("rU", ub), "cc1u%d" % ub)
        for qb in range(4):
            slot = load_w(2048 + qb * 512)
            for hl in range(4):
                h = qb * 4 + hl
                for ci in range(2):
                    c0, cl = CH[ci]
                    qk_unit(slot, hl, ci, qkg[:, 2:3], qT[:, h, c0:c0 + cl], ("qT", h, ci), True)
        advance(flush=True)
        h2s_v = h2s.ap().rearrange("p (k t) -> p k t", t=NX)
        dma("sp", h2s_v, hT[:, :, 0:NX], [("hT", k, c) for k in range(DK) for c in range(3)], ["h2s"], "h2spill")
        P.barrier()

        KT = V(OX, [128, 4, 2304], BF16)
        Vf = V(OX + 18432, [128, 18, 512], BF16)
        pT = [V(OX + 36864 + i * 1024, [128, 512], BF16) for i in range(3)]
        rden = [V(OX + 39936 + i * 2048, [128, 512], F32) for i in range(2)]
        for r_ in range(2):
            rk = rK[r_ * 576:(r_ + 1) * 576, :].rearrange("(h r) c -> h (r c)", h=4).rearrange("h (d t) -> h d t", d=128)
            for h in range(4):
                dma("sp", KT[:, h, r_ * NTOK:(r_ + 1) * NTOK], rk[h], ["rK"], [("KT", h)], ("ldk", r_, h))
            rv = rV[r_ * 576:(r_ + 1) * 576, :].rearrange("(t r) c -> t (r c)", t=9).rearrange("t (p c) -> p t c", p=128)
            dma("sp", Vf[:, r_ * 9:(r_ + 1) * 9, :], rv, ["rV"], ["Vf"], ("ldv", r_))
        na = 0
        for h in range(16):
            kvh = h // 4
            for qc in range(2):
                ob = 2 + (na % 2)
                db = 4 + (na % 2)
                q_ap = qT[:, h, qc * 512:(qc + 1) * 512]

                def smm(kt, h=h, kvh=kvh, qc=qc, q_ap=q_ap):
                    sbk = kt % 2
                    mm(ps[sbk][:, :], KT[:, kvh, kt * 128:(kt + 1) * 128], q_ap, True, True,
                       [("KT", kvh), ("qT", h, qc)], [("ps", sbk)])
                smm(0)
                for kt in range(18):
                    if kt + 1 < 18:
                        smm(kt + 1)
                    sbk = kt % 2
                    pt = pT[kt % 3]
                    actv(pt[:, :], ps[sbk][:, :], AF.Exp, [("ps", sbk)], [("pT", kt % 3)])
                    mm(ps[ob][:, :], Vf[:, kt, kvh * 128:(kvh + 1) * 128], pt[:, :], kt == 0, kt == 17, ["Vf", ("pT", kt % 3)], [("ps", ob)])
                    mm(ps[db][:, :], ones1[:, :], pt[:, :], kt == 0, kt == 17, ["ones1", ("pT", kt % 3)], [("ps", db)])
                rd = rden[na % 2]
                recip(rd[:, :], ps[db][:, :], [("ps", db)], [("rden", na % 2)])
                tt("dve", q_ap, ps[ob][:, :], rd[:, :], ALU.mult, [("ps", ob), ("rden", na % 2)], [("qT", h, qc)])
                na += 1
        P.barrier()
        if dbg and stage == 2:
            d_at = dram_out("d_attn", [128, 16 * NX], BF16)
            dma("sp", d_at[:, :], qT[:, :, :].rearrange("p a b -> p (a b)"), [("qT", h, c) for h in range(16) for c in range(2)], [], "dbg4")
            P.final_waits += ["dbg4"]

    if stage >= 3:
        LW = 2304
        ssmA_d = dram_in("ssmA", [128, 192])
        ssmB_d = dram_in("ssmB", [128, 2048])
        ssmC_d = dram_in("ssmC", [128, 2048])
        ssmD_d = dram_in("ssmD", [128, 4])
        send2 = nc.dram_tensor("send2", [1024, 1024], BF16)
        recv2 = nc.dram_tensor("recv2", [2048, 1024], BF16)
        identF = sb("identF", [128, 128], F32)
        identB = sb("identB", [128, 128], BF16)
        mask8 = sb("mask8", [128, 8], F32)
        ssmD = sb("ssmD_sb", [128, 4], F32)
        rdec = sb("rdec", [128, 64], F32)
        phi = sb("phi", [128, 64], F32)
        psi = sb("psi", [128, 64], F32)
        q25 = sb("q25", [128, 1], F32)
        mset("pool", q25[:, :], 1.0, ["q25"])
        mset("pool", identF[:, :], 1.0, ["identF"])
        asel(identF[:, :], [[-1, 128]], ALU.is_equal, 0, 1, ["identF"], ["identF"])
        cpy("pool", identB[:, :], identF[:, :], ["identF"], ["identB"])
        mset("pool", mask8[:, :], 1.0, ["mask8"])
        asel(mask8[:, :], [[-16, 8]], ALU.is_ge, 0, 1, ["mask8"], ["mask8"])
        asel(mask8[:, :], [[16, 8]], ALU.is_ge, 15, -1, ["mask8"], ["mask8"])
        dma("sp", ssmD[:, :], ssmD_d[:, :], [], ["ssmD"], "small_ssmD")

        uTf = V(OS, [128, 4, LW], BF16)
        Atab = V(OS + 18432, [128, LW], BF16)
        Btab = V(OS + 23040, [128, LW], BF16)
        Gm = V(OS + 27648, [128, 64, 16], F32)
        Gp = V(OS + 31744, [128, 64, 16], F32)
        Em = V(OS + 35840, [128, 64, 16], BF16)
        Ep = V(OS + 37888, [128, 64, 16], BF16)
        BTp = V(OS + 39936, [128, 8, 128], BF16)
        BTq = V(OS + 41984, [128, 8, 128], BF16)
        ETp = V(OS + 44032, [128, 8, 128], BF16)
        ETq = V(OS + 46080, [128, 8, 128], BF16)
        M1 = V(OS + 48128, [128, 2048], BF16)
        M2 = V(OS + 52224, [128, 2048], BF16)
        Ddiag = V(OS + 56320, [128, 4, 128], BF16)

        o = OX
        sA = V(o, [128, 3, 64], F32); o += 768
        sB = V(o, [128, 2, 64, 16], F32); o += 8192
        sC = V(o, [128, 2, 64, 16], F32); o += 8192
        T = [V(o + i * 4096, [128, 64, 16], F32) for i in range(4)]; o += 16384
        sm = [V(o + i * 256, [128, 64], F32) for i in range(16)]; o += 4096
        smi = V(o, [128, 64], I32); o += 256
        tabi = V(o, [128, LW], I32); o += 9216
        dma("sp", sA[:, :, :].rearrange("p a b -> p (a b)"), ssmA_d[:, :], [], ["sA"], "small_sA")
        dma("sp", sB[:, :, :, :].rearrange("p a b c -> p (a b c)"), ssmB_d[:, :], [], ["sB"], "small_sB")
        dma("sp", sC[:, :, :, :].rearrange("p a b c -> p (a b c)"), ssmC_d[:, :], [], ["sC"], "small_sC")
        are, aim = sA[:, 0, :], sA[:, 1, :]
        dt_, adt, th, sin1, cos1, lbr, lbi, lm1, den, cre, cim, tA, tB, tC = sm[0:14]
        K_ = ["ssmset"]
        actv(dt_, sA[:, 2, :], AF.Exp, ["sA"], K_)
        tt("dve", adt, are, dt_, ALU.mult, K_ + ["sA"], K_)
        tt("dve", th, aim, dt_, ALU.mult, K_, K_)
        actv(rdec[:, :], adt, AF.Exp, K_, ["rdec"])
        tsc("dve", phi[:, :], th, 1.0 / (2 * math.pi), None, ALU.mult, None, K_, ["phi"])
        tsc("dve", tA, phi[:, :], 64.0, None, ALU.mult, None, ["phi"], K_)
        cpy("dve", smi, tA, K_, K_)
        tt("dve", psi[:, :], tA, smi, ALU.subtract, K_, ["psi"])
        cpy("dve", smi, phi[:, :], K_ + ["phi", "psi"], K_)
        tt("dve", tB, phi[:, :], smi, ALU.subtract, K_, K_)
        actv(sin1, tB, AF.Sin, K_, K_, scale=2 * math.pi)
        tsc("dve", tC, phi[:, :], 0.25, None, ALU.add, None, K_, K_)
        cpy("dve", smi, tC, K_, K_)
        tt("dve", tB, tC, smi, ALU.subtract, K_, K_)
        actv(cos1, tB, AF.Sin, K_, K_, scale=2 * math.pi)
        tt("dve", lbr, rdec[:, :], cos1, ALU.mult, K_ + ["rdec"], K_)
        tt("dve", lbi, rdec[:, :], sin1, ALU.mult, K_, K_)
        tsc("dve", lm1, lbr, -1.0, None, ALU.add, None, K_, K_)
        tt("dve", den, are, are, ALU.mult, K_, K_)
        tt("dve", tA, aim, aim, ALU.mult, K_, K_)
        tt("dve", den, den, tA, ALU.add, K_, K_)
        recip(den, den, K_, K_)
        tt("dve", tA, lm1, are, ALU.mult, K_, K_)
        tt("dve", tB, lbi, aim, ALU.mult, K_, K_)
        tt("dve", tA, tA, tB, ALU.add, K_, K_)
        tt("dve", cre, tA, den, ALU.mult, K_, K_)
        tt("dve", tA, lbi, are, ALU.mult, K_, K_)
        tt("dve", tB, lm1, aim, ALU.mult, K_, K_)
        tt("dve", tA, tA, tB, ALU.subtract, K_, K_)
        tt("dve", cim, tA, den, ALU.mult, K_, K_)
        Br, Bi = sB[:, 0, :, :], sB[:, 1, :, :]
        Cr, Ci = sC[:, 0, :, :], sC[:, 1, :, :]
        tt("pool", T[0][:, :, :], Br, bc_last(cre, 16), ALU.mult, K_ + ["sB"], ["T0"])
        tt("pool", T[1][:, :, :], Bi, bc_last(cim, 16), ALU.mult, K_ + ["sB"], ["T1"])
        tt("pool", T[2][:, :, :], Bi, bc_last(cre, 16), ALU.mult, K_ + ["sB"], ["T2"])
        tt("pool", T[3][:, :, :], Br, bc_last(cim, 16), ALU.mult, K_ + ["sB"], ["T3"])
        tt("dve", Gm[0:64], T[0][0:64], T[1][0:64], ALU.subtract, ["T0", "T1"], ["Gm0"])
        tt("dve", Gm[64:128], T[2][64:128], T[3][64:128], ALU.add, ["T2", "T3"], ["Gm1"])
        tt("dve", Gp[0:64], T[2][0:64], T[3][0:64], ALU.add, ["T2", "T3"], ["Gp0"])
        tt("dve", Gp[64:128], T[1][64:128], T[0][64:128], ALU.subtract, ["T0", "T1"], ["Gp1"])
        cpy("pool", Em[0:64], Cr[0:64], ["sC"], ["Em0"])
        tsc("pool", Em[64:128], Ci[64:128], -1.0, None, ALU.mult, None, ["sC"], ["Em1"])
        tsc("pool", Ep[0:64], Ci[0:64], -1.0, None, ALU.mult, None, ["sC"], ["Ep0"])
        tsc("pool", Ep[64:128], Cr[64:128], -1.0, None, ALU.mult, None, ["sC"], ["Ep1"])
        P.op("pool", lambda e: e.iota(tabi[:, :].rearrange("p (a b) -> p a b", b=64), pattern=[[1, 36], [0, 64]], base=0, channel_multiplier=0), [], ["tabi"])
        cpy("pool", Atab[:, :], tabi[:, :], ["tabi"], ["Atab"])
        P.op("pool", lambda e: e.iota(tabi[:, :].rearrange("p (a b) -> p a b", b=64), pattern=[[0, 36], [1, 64]], base=0, channel_multiplier=0), ["Atab"], ["tabi"])
        cpy("pool", Btab[:, :], tabi[:, :], ["tabi"], ["Btab"])
        for ct in range(4):
            tsc("dve", Ddiag[:, ct, :], identF[:, :], ssmD[:, ct:ct + 1], None, ALU.mult, None, ["identF", "ssmD"], [("Ddiag", ct)])
        P.barrier()

        stA = V(OX, [128, NTOK], BF16)
        stB = V(OX + 2304, [128, NTOK], BF16)
        for r_ in range(2):
            ru = [rU[ub][r_ * 576:(r_ + 1) * 576, :].rearrange("(a r) c -> a (r c)", a=4).rearrange("a (p t) -> a p t", p=128) for ub in range(2)]
            for i in range(4):
                dma("sp", stA[:, :], ru[0][i], [("rU", 0)], ["stA"], "ldsta")
                dma("sp", stB[:, :], ru[1][i], [("rU", 1)], ["stB"], "ldstb")
                tsc("dve", stA[:, :], stA[:, :], cinfo[:, 1:2], None, ALU.mult, None, ["stA", "cinfo"], ["stA"])
                stt("dve", uTf[:, i, 256 + r_ * 1024:256 + (r_ + 1) * 1024], stB[:, 0:1024], cinfo[:, 0:1], stA[:, 0:1024],
                    ALU.mult, ALU.add, ["stA", "stB", "cinfo"], [("uTf", i, "x", r_)])
                stt("dve", uTf[:, i, r_ * 128:(r_ + 1) * 128], stB[:, 1024:1152], cinfo[:, 0:1], stA[:, 1024:1152],
                    ALU.mult, ALU.add, ["stA", "stB", "cinfo"], [("uTf", i, "c", r_)])
        P.barrier()
        uTf_keys = [("uTf", i, a, r_) for i in range(4) for a in ("x", "c") for r_ in range(2)]

        tA = V(OX, [128, LW], F32)
        ki_ = V(OX + 18432, [128, LW], I32)
        tB = V(OX + 18432, [128, LW], F32)
        SINb = [V(OX + 27648, [128, LW], F32), V(OH, [128, LW], F32)]
        COSb = [V(OX + 36864, [128, LW], F32), V(OH + 9216, [128, LW], F32)]
        zmd = V(OH + 18432, [128, LW], F32)
        gsc = V(OH + 27648, [128, LW], BF16)
        SCb = [V(OX + 9216, [128, 2, 2048], BF16), V(OX + 64512, [128, 2, 2048], BF16)]
        tm1 = [V(OX + 46080 + i * 2048, [128, 512], F32) for i in range(2)]
        tm2 = [V(OX + 50176 + i * 2048, [128, 512], F32) for i in range(2)]
        yc = V(OX + 54272, [128, 512], F32)
        y2 = V(OX + 56320, [128, 512], F32)
        y3 = V(OX + 58368, [128, 512], F32)
        ystg = [V(OX + 60416, [128, 2048], BF16)] * 2
        s2v = send2.ap().rearrange("(a r) c -> a (r c)", a=4).rearrange("a (p t) -> a p t", p=128)

        WCH = [(0, 256)] + [(256 + i * 512, 512) for i in range(4)]
        cnt3 = dict(nz=0)
        units = [(ct, d_, g8) for ct in range(4) for d_ in range(2) for g8 in range(8)]

        def tables(u, i):
            ct, d_, g8 = units[u]
            dg = d_ * 32 + ct * 8 + g8
            SIN, COS = SINb[i], COSb[i]
            ks, kc = ("SIN", i), ("COS", i)
            actv(tA[:, :], Atab[:, :], AF.Identity, ["Atab", "psi"], ["tA"], scale=psi[:, dg:dg + 1])
            stt("dve", tA[:, :], Btab[:, :], phi[:, dg:dg + 1], tA[:, :], ALU.mult, ALU.add, ["Btab", "phi", "tA"], ["tA"])
            cpy("dve", ki_[:, :], tA[:, :], ["tA"], ["ki_"])
            tt("dve", tB[:, :], tA[:, :], ki_[:, :], ALU.subtract, ["tA", "ki_"], ["tB", "ki_"])
            actv(SIN[:, :], tB[:, :], AF.Sin, ["tB", "ki_"], [ks], scale=2 * math.pi)
            actv(COS[:, :], tB[:, :], AF.Sin, ["tB", "ki_"], [kc], scale=math.pi)
            actv(COS[:, :], COS[:, :], AF.Square, [kc], [kc])
            actv(COS[:, :], COS[:, :], AF.Identity, [kc, "q25"], [kc], scale=-2.0, bias=q25[:, 0:1])
            cpy("act", SCb[i][:, 0, :], COS[:, 256:LW], [kc], [("SCb", i)])
            cpy("act", SCb[i][:, 1, :], SIN[:, 256:LW], [ks], [("SCb", i)])

        def setup_ctd(ct, d_):
            dg0 = d_ * 32 + ct * 8
            mm(ps[0][:, 0:128], Gm[:, dg0:dg0 + 8, :].rearrange("p a b -> p (a b)"), identF[:, :], True, True,
               ["Gm0", "Gm1", "identF"], [("ps", 0)])
            mm(ps[1][:, 0:128], Gp[:, dg0:dg0 + 8, :].rearrange("p a b -> p (a b)"), identF[:, :], True, True,
               ["Gp0", "Gp1", "identF"], [("ps", 1)])
            for g8 in range(8):
                tsc("dve", BTp[:, g8, :], ps[0][:, 0:128], mask8[:, g8:g8 + 1], None, ALU.mult, None, [("ps", 0), "mask8"], [("BTp", g8)])
                tsc("dve", BTq[:, g8, :], ps[1][:, 0:128], mask8[:, g8:g8 + 1], None, ALU.mult, None, [("ps", 1), "mask8"], [("BTq", g8)])
            mset("pool", ETp[:, :, :], 0.0, [("ETp", g8) for g8 in range(8)])
            mset("pool", ETq[:, :, :], 0.0, [("ETq", g8) for g8 in range(8)])
            etp_d = bass.AP(ETp.tensor, ETp.offset, [list(ETp.ap[0]), [144, 8], [1, 16]])
            etq_d = bass.AP(ETq.tensor, ETq.offset, [list(ETq.ap[0]), [144, 8], [1, 16]])
            cpy("pool", etp_d, Em[:, dg0:dg0 + 8, :], ["Em0", "Em1"] + [("ETp", g8) for g8 in range(8)], [("ETp", g8) for g8 in range(8)])
            cpy("pool", etq_d, Ep[:, dg0:dg0 + 8, :], ["Ep0", "Ep1"] + [("ETq", g8) for g8 in range(8)], [("ETq", g8) for g8 in range(8)])

        def main_unit(u, i):
            ct, d_, g8 = units[u]
            dg = d_ * 32 + ct * 8 + g8
            SIN, COS = SINb[i], COSb[i]
            ks, kc = ("SIN", i), ("COS", i)
            for (n0, ln) in WCH:
                zb = (cnt3["nz"] % 2) * 2
                cnt3["nz"] += 1
                mm(ps[zb][:, 0:ln], BTp[:, g8, :], uTf[:, ct, n0:n0 + ln], True, True, [("BTp", g8)] + uTf_keys, [("ps", zb)])
                mm(ps[zb + 1][:, 0:ln], BTq[:, g8, :], uTf[:, ct, n0:n0 + ln], True, True, [("BTq", g8)] + uTf_keys, [("ps", zb + 1)])
                if d_ == 0:
                    cs, sn, zo = COS[:, n0:n0 + ln], SIN[:, n0:n0 + ln], zmd[:, n0:n0 + ln]
                else:
                    mlo = (255 - (n0 + ln - 1)) if n0 < 256 else (2559 - (n0 + ln - 1))
                    cs, sn, zo = rev(COS[:, mlo:mlo + ln]), rev(SIN[:, mlo:mlo + ln]), rev(zmd[:, mlo:mlo + ln])
                i2 = cnt3["nz"] % 2
                tt("dve", tm1[i2][:, 0:ln], ps[zb][:, 0:ln], cs, ALU.mult, [("ps", zb), kc], [("tm1", i2)])
                tt("dve", tm2[i2][:, 0:ln], ps[zb + 1][:, 0:ln], sn, ALU.mult, [("ps", zb + 1), ks], [("tm2", i2)])
                tt("dve", zo, tm1[i2][:, 0:ln], tm2[i2][:, 0:ln], ALU.add, [("tm1", i2), ("tm2", i2)], ["zmd"])
            P.op("dve", lambda e: e.tensor_tensor_scan(out=gsc[:, :], data0=rdec[:, dg:dg + 1].to_broadcast([128, LW]),
                                                       data1=zmd[:, :], initial=0.0, op0=ALU.mult, op1=ALU.add),
                 ["zmd", "rdec"], ["gsc"])
            tt("dve", M1[:, :], gsc[:, 256:LW], SCb[i][:, 0, :], ALU.mult, ["gsc", ("SCb", i)], ["M1"])
            tt("dve", M2[:, :], gsc[:, 256:LW], SCb[i][:, 1, :], ALU.mult, ["gsc", ("SCb", i)], ["M2"])
            first = (d_ == 0 and g8 == 0)
            for xc in range(4):
                if d_ == 0:
                    r1, r2 = M1[:, xc * 512:(xc + 1) * 512], M2[:, xc * 512:(xc + 1) * 512]
                else:
                    lo = 2048 - (xc + 1) * 512
                    r1, r2 = rev(M1[:, lo:lo + 512]), rev(M2[:, lo:lo + 512])
                if first:
                    mm(ps[4 + xc][:, :], Ddiag[:, ct, :], uTf[:, ct, 256 + xc * 512:256 + (xc + 1) * 512], True, False,
                       [("Ddiag", ct)] + uTf_keys, [("ps", 4 + xc)])
                mm(ps[4 + xc][:, :], ETp[:, g8, :], r1, False, False, [("ETp", g8), "M1"], [("ps", 4 + xc)])
                mm(ps[4 + xc][:, :], ETq[:, g8, :], r2, False, (d_ == 1 and g8 == 7), [("ETq", g8), "M2"], [("ps", 4 + xc)])

        def finish_ct(ct):
            yst = ystg[0]
            for xc in range(4):
                cpy("act", yc[:, :], ps[4 + xc][:, :], [("ps", 4 + xc)], ["yc"])
                tt("dve", y2[:, :], yc[:, :], yc[:, :], ALU.mult, ["yc"], ["y2"])
                tsc("dve", y2[:, :], y2[:, :], 0.044715, 1.0, ALU.mult, ALU.add, ["y2"], ["y2"])
                tt("dve", y2[:, :], y2[:, :], yc[:, :], ALU.mult, ["y2", "yc"], ["y2"])
                actv(y3[:, :], y2[:, :], AF.Sigmoid, ["y2"], ["y3"], scale=2.0 * math.sqrt(2.0 / math.pi))
                tt("dve", yst[:, xc * 512:(xc + 1) * 512], y3[:, :], yc[:, :], ALU.mult, ["y3", "yc"], [("ystg", 0)])
            dma("sp", s2v[ct], yst[:, :], [("ystg", 0)], [("send2", ct)], ("send2", 0))

        tables(0, 0)
        for u in range(len(units)):
            ct, d_, g8 = units[u]
            if u + 1 < len(units):
                tables(u + 1, (u + 1) % 2)
            if g8 == 0:
                setup_ctd(ct, d_)
            main_unit(u, u % 2)
            if d_ == 1 and g8 == 7:
                finish_ct(ct)
        P.coll(lambda e: e.collective_compute("AllGather", ALU.bypass, replica_groups=PAIRS,
                                              ins=[send2.ap().opt()], outs=[recv2.ap().opt()]),
               [("send2", ct) for ct in range(4)], ["recv2"], "cc2")
        P.barrier()

        ygT = V(OS, [128, 8, NX], BF16)
        ga = V(OX, [128, NX], BF16)
        gb = V(OX + 2048, [128, NX], BF16)
        r2v = recv2.ap().rearrange("(a r) c -> a (r c)", a=8).rearrange("a (p t) -> a p t", p=128)
        for a in range(8):
            dma("sp", ga[:, :], r2v[a][:, 0:NX], ["recv2"], ["ga"], "ldga")
            dma("sp", gb[:, :], r2v[a][:, NX:2 * NX], ["recv2"], ["gb"], "ldgb")
            tsc("dve", ga[:, :], ga[:, :], cinfo[:, 1:2], None, ALU.mult, None, ["ga", "cinfo"], ["ga"])
            stt("dve", ygT[:, a, :], gb[:, :], cinfo[:, 0:1], ga[:, :], ALU.mult, ALU.add, ["ga", "gb", "cinfo"], [("ygT", a)])
        P.barrier()
        if dbg and stage == 3:
            d_yg = dram_out("d_yg", [128, 8 * NX], BF16)
            dma("sp", d_yg[:, :], ygT[:, :, :].rearrange("p a b -> p (a b)"), [("ygT", a) for a in range(8)], [], "dbg5")
            P.final_waits += ["dbg5"]

    if stage >= 4:
        wglu_d = dram_in("w_glu", [1024, 1024])
        bglu_d = dram_in("b_gluT", [128, 8])
        wbra_d = dram_in("w_br_attn", [D, D])
        wbrs_d = dram_in("w_br_ssm", [1024, D])
        wout_d = dram_in("w_out", [D, D])
        bglu = sb("bglu_sb", [128, 8], F32)
        dma("sp", bglu[:, :], bglu_d[:, :], [], ["bglu"], "small_bglu")
        mrgT = V(OS + 16384, [128, 16, NX], BF16)
        XCH = [0, 1]
        wgl = V(OX, [128, 8, 1024], BF16)
        gt = [V(OX + 16384 + i * 2048, [128, 512], F32) for i in range(2)]
        y2T = V(OX + 20480, [128, 8, NX], BF16)
        dma("pool", wgl[:, :, :], wglu_d.rearrange("(k p) f -> p k f", p=128), [], ["wgl"], "wgl")
        n4 = 0
        for a in range(8):
            for ci in XCH:
                c0, cl = CH[ci]
                b = n4 % 2
                n4 += 1
                for k in range(8):
                    mm(ps[b][:, :], wgl[:, k, a * 128:(a + 1) * 128], ygT[:, k, c0:c0 + cl], k == 0, k == 7, ["wgl"] + [("ygT", k)], [("ps", b)])
                actv(gt[b][:, :], ps[b][:, :], AF.Sigmoid, [("ps", b), "bglu"], [("gt", b)], bias=bglu[:, a:a + 1], scale=1.0)
                tt("dve", y2T[:, a, c0:c0 + cl], gt[b][:, :], ygT[:, a, c0:c0 + cl], ALU.mult, [("gt", b), ("ygT", a)], [("y2T", a)])
        P.barrier()
        dma("sp", hT[:, :, 0:NX], h2s_v, ["h2s"], [("hT", k, c) for k in range(DK) for c in range(3)], "h2load")
        o = OX + 36864
        wA = V(o, [128, 16, 256], BF16)
        wS = V(o + 8192, [128, 8, 256], BF16)
        wGa = V(o + 12288, [128, 16, 256], BF16)
        wGs = V(o + 20480, [128, 16, 256], BF16)
        m1 = [V(o + 28672 + i * 2048, [128, 512], F32) for i in range(2)]
        m2 = [V(o + 32768 + i * 2048, [128, 512], F32) for i in range(2)]
        wbra_v = wbra_d.rearrange("(k p) f -> p k f", p=128)
        wbrs_v = wbrs_d.rearrange("(k p) f -> p k f", p=128)
        for db2 in range(8):
            c_ = db2 * 256
            dma("pool", wA[:, :, :], wbra_v[:, :, c_:c_ + 256], [], ["wA"], "wA")
            dma("pool", wS[:, :, :], wbrs_v[:, :, c_:c_ + 256], [], ["wS"], "wS")
            dma("pool", wGa[:, :, :], win_v[:, :, 4096 + c_:4096 + c_ + 256], [], ["wGa"], "wGa")
            dma("pool", wGs[:, :, :], win_v[:, :, 6144 + c_:6144 + c_ + 256], [], ["wGs"], "wGs")
            for dl in range(2):
                dk = db2 * 2 + dl
                for ci in XCH:
                    c0, cl = CH[ci]
                    i = n4 % 2
                    n4 += 1
                    bA, bS, bGa, bGs = (0, 1, 2, 3) if i == 0 else (4, 5, 6, 7)
                    for k in range(16):
                        mm(ps[bA][:, :], wA[:, k, dl * 128:(dl + 1) * 128], qT[:, k, c0:c0 + cl], k == 0, k == 15, ["wA", ("qT", k, ci)], [("ps", bA)])
                    for k in range(8):
                        mm(ps[bS][:, :], wS[:, k, dl * 128:(dl + 1) * 128], y2T[:, k, c0:c0 + cl], k == 0, k == 7, ["wS", ("y2T", k)], [("ps", bS)])
                    for k in range(16):
                        mm(ps[bGa][:, :], wGa[:, k, dl * 128:(dl + 1) * 128], hT[:, k, c0:c0 + cl], k == 0, k == 15, ["wGa", ("hT", k, ci)], [("ps", bGa)])
                    for k in range(16):
                        mm(ps[bGs][:, :], wGs[:, k, dl * 128:(dl + 1) * 128], hT[:, k, c0:c0 + cl], k == 0, k == 15, ["wGs", ("hT", k, ci)], [("ps", bGs)])
                    actv(m1[i][:, :], ps[bGa][:, :], AF.Sigmoid, [("ps", bGa)], [("m1", i)])
                    actv(m2[i][:, :], ps[bGs][:, :], AF.Sigmoid, [("ps", bGs)], [("m2", i)])
                    tt("dve", m1[i][:, :], m1[i][:, :], ps[bA][:, :], ALU.mult, [("m1", i), ("ps", bA)], [("m1", i)])
                    tt("dve", m2[i][:, :], m2[i][:, :], ps[bS][:, :], ALU.mult, [("m2", i), ("ps", bS)], [("m2", i)])
                    tt("pool", mrgT[:, dk, c0:c0 + cl], m1[i][:, :], m2[i][:, :], ALU.add, [("m1", i), ("m2", i)], [("mrgT", dk, ci)])
        P.barrier()
        xT2 = V(OX, [128, DK, NX], F32)
        wO = [V(OH + i * 8192, [128, 16, 256], BF16) for i in range(2)]
        xs = [V(OH + 16384 + i * 2048, [128, 512], F32) for i in range(2)]
        wout_v = wout_d.rearrange("(k p) f -> p k f", p=128)
        for db2 in range(8):
            sl = db2 % 2
            dma("pool", wO[sl][:, :, :], wout_v[:, :, db2 * 256:(db2 + 1) * 256], [], [("wO", sl)], ("wO", sl))
            for dl in range(2):
                dk = db2 * 2 + dl
                for ci in XCH:
                    c0, cl = CH[ci]
                    i = n4 % 2
                    n4 += 1
                    dma("sp", xs[i][:, :], x1s_v[:, dk, c0:c0 + cl], [("x1s", dk // 4)], [("xs", i)], ("xs", i))
                    for k in range(16):
                        mm(ps[i][:, :], wO[sl][:, k, dl * 128:(dl + 1) * 128], mrgT[:, k, c0:c0 + cl], k == 0, k == 15,
                           [("wO", sl), ("mrgT", k, ci)], [("ps", i)])
                    stt("dve", xT2[:, dk, c0:c0 + cl], ps[i][:, :], prm[:, G2, dk:dk + 1], xs[i][:, :], ALU.mult, ALU.add,
                        [("ps", i), ("xs", i), ("prm", G2)], [("xT", dk, ci)])
        P.barrier()
        final_x = xT2
        if dbg and stage == 4:
            pass

    if stage >= 5:
        w2g_d = dram_in("w_ffn2_gate", [D, DFF])
        w2u_d = dram_in("w_ffn2_up", [D, DFF])
        w2d_d = dram_in("w_ffn2_down", [DFF, D])
        hT3 = V(OH, [128, DK, NX], BF16)
        norm_mod("n3", xT2, hT3, A3, B3, A3, B3, [0, 1])
        P.barrier()
        ffn("f2", xT2, hT3, NX, w2g_d, w2u_d, w2d_d, G3, G3, [0, 1])
        P.barrier()

    out_v = out_d.rearrange("(k p) t -> p k t", p=128)
    for k4 in range(4 if (stage == 1 or stage >= 4) else 0):
        dma("sp", out_v[:, 4 * k4:4 * k4 + 4, :], final_x[:, 4 * k4:4 * k4 + 4, 0:NX],
            [("xT", k, c) for k in range(4 * k4, 4 * k4 + 4) for c in range(3)], [], "ostore")
    if stage == 1 or stage >= 4:
        P.final_waits.append("ostore")

    P.emit()
    es.close()
    return nc


def prep_inputs(inputs, cores, stage=99):
    x = inputs["x"]; ctx = inputs["ctx"]; c = inputs["c"]; c_ctx = inputs["c_ctx"]
    maps = []
    ca = np.ascontiguousarray
    ng = ca(inputs["norm_g"][0].reshape(3, 16, 128).transpose(2, 0, 1).reshape(128, 48))
    shared = {
        "w_mod": ca(inputs["w_mod"][0]),
        "b_modT": ca(inputs["b_mod"][0].reshape(144, 128).T),
        "ng": ng,
        "w_ffn1_gate": ca(inputs["w_ffn1_gate"][0]),
        "w_ffn1_up": ca(inputs["w_ffn1_up"][0]),
        "w_ffn1_down": ca(inputs["w_ffn1_down"][0]),
    }
    if stage >= 2:
        shared["w_in"] = ca(inputs["w_in"][0])
        shared["qkg"] = ca(np.stack([inputs["q_norm_g"][0], inputs["k_norm_g"][0]], axis=1))
    if stage >= 4:
        shared["w_glu"] = ca(inputs["w_glu"][0])
        shared["b_gluT"] = ca(inputs["b_glu"][0].reshape(8, 128).T)
        shared["w_br_attn"] = ca(inputs["w_br_attn"][0])
        shared["w_br_ssm"] = ca(inputs["w_br_ssm"][0])
        shared["w_out"] = ca(inputs["w_out"][0])
    if stage >= 5:
        shared["w_ffn2_gate"] = ca(inputs["w_ffn2_gate"][0])
        shared["w_ffn2_up"] = ca(inputs["w_ffn2_up"][0])
        shared["w_ffn2_down"] = ca(inputs["w_ffn2_down"][0])
    ssm = {}
    if stage >= 3:
        for j in range(2):
            gs = slice(32 * j, 32 * j + 32)

            def pdg(a):
                t = a.transpose(2, 0, 1).reshape(64, 64)
                return np.concatenate([t, t], 0)
            are = pdg(inputs["ssm_a_re"][0][:, gs, :])
            aim = pdg(inputs["ssm_a_im"][0][:, gs, :])
            ldt = np.broadcast_to(inputs["ssm_log_dt"][0][:, gs].reshape(1, 64), (128, 64))
            sA = np.stack([are, aim, ldt], 1).reshape(128, 192)

            def pB(a):
                t = a.transpose(2, 0, 1, 3).reshape(64, 64, 16)
                return np.concatenate([t, t], 0)
            sB = np.stack([pB(inputs["ssm_b_re"][0][:, gs]), pB(inputs["ssm_b_im"][0][:, gs])], 1).reshape(128, 2048)

            def pC(a):
                t = a.transpose(3, 0, 1, 2).reshape(64, 64, 16)
                return np.concatenate([t, t], 0)
            sC = np.stack([pC(inputs["ssm_c_re"][0][:, gs]), pC(inputs["ssm_c_im"][0][:, gs])], 1).reshape(128, 2048)
            sD = inputs["ssm_d"][0][512 * j:512 * (j + 1)].reshape(4, 128).T
            ssm[j] = dict(ssmA=ca(sA.astype(np.float32)), ssmB=ca(sB), ssmC=ca(sC), ssmD=ca(sD))
    for core in cores:
        b, j = core // 2, core % 2
        xt = np.concatenate([x[b, j * 1024:(j + 1) * 1024], ctx[b, j * 128:(j + 1) * 128]], axis=0)
        cv = np.stack([c[b].reshape(16, 128).T, c_ctx.reshape(16, 128).T], axis=-1).reshape(128, 32)
        m = dict(shared)
        m["xT"] = ca(xt.T)
        m["cvec"] = ca(cv)
        m["cinfo"] = ca(np.tile(np.array([[j, 1 - j, 16 * j, 0]], np.float32), (128, 1)))
        if stage >= 3:
            m.update(ssm[j])
        maps.append(m)
    return maps


def kernel(**inputs):
    inputs = {k: np.asarray(v) for k, v in inputs.items()}
    cores = list(range(N_CORES))
    nc = build_nc()
    maps = prep_inputs(inputs, cores)
    res = run_bass_kernel_spmd(nc, maps, core_ids=cores)
    out = np.zeros((4, 2048, 2048), np.float32)
    for core in cores:
        b, j = core // 2, core % 2
        o = res.results[core]["outT"]
        out[b, j * 1024:(j + 1) * 1024] = o.T
    return out
```
